# Optimizing a Trainium2 kernel written in Bass

```python
import math
import jax, jax.numpy as jnp
from jax import lax
import numpy as np

D_MODEL = 1024
BATCH = 8
SEQ = 4096
DEPTH = 4

MIX_WIDTH = D_MODEL
SGU_HEADS = 4
SGU_HEAD_DIM = 128
SGU_WIDTH = SGU_HEADS * SGU_HEAD_DIM
SGU_CHUNK = 128
SGU_W_STD = 0.05
GDN_HEADS = 4
GDN_DK = 128
GDN_DV = 128
GDN_QK_WIDTH = GDN_HEADS * GDN_DK
GDN_V_WIDTH = GDN_HEADS * GDN_DV
GDN_CHUNK = 64
CONV_WIDTH = 4
CONV_CHANNELS = 2 * GDN_QK_WIDTH + GDN_V_WIDTH
D_FF = 4 * D_MODEL
N_MOD = 6
RMS_EPS = 1e-6
LN_EPS = 1e-5
IN_SIZES = (SGU_WIDTH, SGU_WIDTH, GDN_QK_WIDTH, GDN_QK_WIDTH, GDN_V_WIDTH, GDN_V_WIDTH, GDN_HEADS, GDN_HEADS)
IN_WIDTH = 2 * SGU_WIDTH + 2 * GDN_QK_WIDTH + 2 * GDN_V_WIDTH + 2 * GDN_HEADS

kernel_name = 'hybrid_sgu_gdn_adaln_trunk'


def rmsnorm(x, g):
    xf = x.astype(jnp.float32)
    y = xf * lax.rsqrt(jnp.mean(xf * xf, axis=-1, keepdims=True) + RMS_EPS)
    return (y * g.astype(jnp.float32)).astype(x.dtype)


def layernorm(x, g, b):
    xf = x.astype(jnp.float32)
    mu = jnp.mean(xf, axis=-1, keepdims=True)
    xc = xf - mu
    y = xc * lax.rsqrt(jnp.mean(xc * xc, axis=-1, keepdims=True) + LN_EPS)
    return (y * g.astype(jnp.float32) + b.astype(jnp.float32)).astype(x.dtype)


def l2norm(x):
    xf = x.astype(jnp.float32)
    return xf * lax.rsqrt(jnp.sum(xf * xf, axis=-1, keepdims=True) + RMS_EPS)


def split_cols(p, sizes):
    out, start = [], 0
    for s in sizes:
        out.append(p[..., start:start + s])
        start += s
    return out


def causal_depthwise_conv(x, w):
    k = w.shape[0]
    return lax.conv_general_dilated(x, w[:, None, :], window_strides=(1,), padding=[(k - 1, 0)],
                                    dimension_numbers=('NWC', 'WIO', 'NWC'),
                                    feature_group_count=x.shape[-1])


def sgu_mixer(u, v, ln_g, ln_b, w_s, b_s):
    bsz, t, _ = u.shape
    n = t // SGU_CHUNK
    v = layernorm(v, ln_g, ln_b)
    vc = v.reshape(bsz, n, SGU_CHUNK, SGU_HEADS, SGU_HEAD_DIM)
    causal = jnp.tril(jnp.ones((SGU_CHUNK, SGU_CHUNK), dtype=bool))
    w = jnp.where(causal[None], w_s, 0.0)
    mixed = jnp.einsum('hts,bnshd->bnthd', w, vc) + b_s.T[None, None, :, :, None]
    return u * mixed.reshape(bsz, t, SGU_WIDTH)


def gated_delta_rule(q, k, v, g, beta):
    bsz, t, h, dk = q.shape
    dv = v.shape[-1]
    c = GDN_CHUNK
    n = t // c
    q = q.reshape(bsz, n, c, h, dk).transpose(0, 3, 1, 2, 4)
    k = k.reshape(bsz, n, c, h, dk).transpose(0, 3, 1, 2, 4)
    v = v.reshape(bsz, n, c, h, dv).transpose(0, 3, 1, 2, 4)
    g = g.reshape(bsz, n, c, h).transpose(0, 3, 1, 2)
    beta = beta.reshape(bsz, n, c, h).transpose(0, 3, 1, 2)
    g_cum = jnp.cumsum(g, axis=-1)
    idx = jnp.arange(c)
    incl = idx[:, None] >= idx[None, :]
    strict = idx[:, None] > idx[None, :]
    diff = g_cum[..., :, None] - g_cum[..., None, :]
    decay = jnp.where(incl, jnp.exp(jnp.where(incl, diff, 0.0)), 0.0)
    kk = jnp.einsum('bhntd,bhnsd->bhnts', k, k)
    m = jnp.where(strict, beta[..., :, None] * kk * decay, 0.0)
    a_mat = jnp.eye(c, dtype=jnp.float32) + m
    gamma = jnp.exp(g_cum)
    rhs = jnp.concatenate([beta[..., None] * v, (beta * gamma)[..., None] * k], axis=-1)
    sol = lax.linalg.triangular_solve(a_mat, rhs, left_side=True, lower=True, unit_diagonal=True)
    u_new = sol[..., :dv]
    w_k = sol[..., dv:]
    qk = jnp.einsum('bhntd,bhnsd->bhnts', q, k) * decay
    q_dec = q * gamma[..., None]
    k_dec = k * jnp.exp(g_cum[..., -1:] - g_cum)[..., None]
    gamma_last = gamma[..., -1]

    def step(s, inp):
        q_d, k_d, u_c, wk_c, a_c, gl = inp
        w = u_c - jnp.einsum('bhck,bhkv->bhcv', wk_c, s)
        o = jnp.einsum('bhck,bhkv->bhcv', q_d, s) + jnp.einsum('bhts,bhsv->bhtv', a_c, w)
        s = gl[..., None, None] * s + jnp.einsum('bhck,bhcv->bhkv', k_d, w)
        return s, o

    xs = tuple(jnp.moveaxis(a, 2, 0) for a in (q_dec, k_dec, u_new, w_k, qk, gamma_last))
    s0 = jnp.zeros((bsz, h, dk, dv), jnp.float32)
    _, o = lax.scan(step, s0, xs)
    return o.transpose(1, 0, 3, 2, 4).reshape(bsz, t, h, dv)


def gdn_mixer(q, k, v, z, b_raw, a_raw, conv_w, a_log, dt_bias, norm_g):
    bsz, t, _ = q.shape
    dtype = q.dtype
    qkv = jax.nn.silu(causal_depthwise_conv(jnp.concatenate([q, k, v], axis=-1), conv_w))
    q, k, v = split_cols(qkv, (GDN_QK_WIDTH, GDN_QK_WIDTH, GDN_V_WIDTH))
    q = l2norm(q.reshape(bsz, t, GDN_HEADS, GDN_DK)) * (GDN_DK ** -0.5)
    k = l2norm(k.reshape(bsz, t, GDN_HEADS, GDN_DK))
    v = v.reshape(bsz, t, GDN_HEADS, GDN_DV).astype(jnp.float32)
    beta = jax.nn.sigmoid(b_raw.astype(jnp.float32))
    g = -jnp.exp(a_log.astype(jnp.float32)) * jax.nn.softplus(a_raw.astype(jnp.float32) + dt_bias.astype(jnp.float32))
    o = gated_delta_rule(q, k, v, g, beta)
    o = rmsnorm(o, norm_g) * jax.nn.silu(z.reshape(bsz, t, GDN_HEADS, GDN_DV).astype(jnp.float32))
    return o.reshape(bsz, t, GDN_V_WIDTH).astype(dtype)


def setup_inputs(seed: int = 0) -> dict:
    key = jax.random.key(seed)
    ks = jax.random.split(key, 20)
    d = D_MODEL

    def nrm(k, shape, std):
        return jax.random.normal(k, shape, jnp.float32) * std

    dt = jnp.exp(jax.random.uniform(ks[12], (DEPTH, GDN_HEADS), jnp.float32,
                                    minval=math.log(1e-3), maxval=math.log(1e-1)))
    return {
        'x': nrm(ks[0], (BATCH, SEQ, d), 1.0),
        'c': nrm(ks[1], (BATCH, d), 1.0),
        'w_ada': nrm(ks[2], (DEPTH, d, N_MOD * d), 0.5 * d ** -0.5),
        'b_ada': nrm(ks[3], (DEPTH, N_MOD * d), 0.01),
        'norm1_g': 1.0 + nrm(ks[4], (DEPTH, d), 0.02),
        'w_in': nrm(ks[5], (DEPTH, d, IN_WIDTH), d ** -0.5),
        'sgu_ln_g': 1.0 + nrm(ks[6], (DEPTH, SGU_WIDTH), 0.02),
        'sgu_ln_b': nrm(ks[7], (DEPTH, SGU_WIDTH), 0.02),
        'sgu_w': nrm(ks[8], (DEPTH, SGU_HEADS, SGU_CHUNK, SGU_CHUNK), SGU_W_STD),
        'sgu_b': 1.0 + nrm(ks[9], (DEPTH, SGU_HEADS, SGU_CHUNK), 0.1),
        'conv_w': nrm(ks[10], (DEPTH, CONV_WIDTH, CONV_CHANNELS), CONV_WIDTH ** -0.5),
        'a_log': jnp.log(jax.random.uniform(ks[11], (DEPTH, GDN_HEADS), jnp.float32, minval=1.0, maxval=16.0)),
        'dt_bias': dt + jnp.log(-jnp.expm1(-dt)),
        'gdn_norm_g': 1.0 + nrm(ks[13], (DEPTH, GDN_DV), 0.02),
        'w_out': nrm(ks[14], (DEPTH, MIX_WIDTH, d), MIX_WIDTH ** -0.5),
        'norm2_g': 1.0 + nrm(ks[15], (DEPTH, d), 0.02),
        'w_ff1': nrm(ks[16], (DEPTH, d, D_FF), d ** -0.5),
        'w_ff2': nrm(ks[17], (DEPTH, D_FF, d), D_FF ** -0.5),
        'final_g': 1.0 + nrm(ks[18], (d,), 0.02),
    }


def reference(x, c, w_ada, b_ada, norm1_g, w_in, sgu_ln_g, sgu_ln_b, sgu_w, sgu_b, conv_w,
              a_log, dt_bias, gdn_norm_g, w_out, norm2_g, w_ff1, w_ff2, final_g):
    c_act = jax.nn.silu(c)
    for l in range(DEPTH):
        mod = (c_act @ w_ada[l] + b_ada[l])[:, None, :]
        sh1, sc1, g1, sh2, sc2, g2 = jnp.split(mod, N_MOD, axis=-1)
        h = rmsnorm(x, norm1_g[l]) * (1.0 + sc1) + sh1
        p = h @ w_in[l]
        u, vs, q, k, v, z, b_raw, a_raw = split_cols(p, IN_SIZES)
        y_sgu = sgu_mixer(jax.nn.gelu(u), jax.nn.gelu(vs), sgu_ln_g[l], sgu_ln_b[l], sgu_w[l], sgu_b[l])
        y_gdn = gdn_mixer(q, k, v, z, b_raw, a_raw, conv_w[l], a_log[l], dt_bias[l], gdn_norm_g[l])
        mix = jnp.concatenate([y_sgu, y_gdn], axis=-1)
        x = x + g1 * (mix @ w_out[l])
        h = rmsnorm(x, norm2_g[l]) * (1.0 + sc2) + sh2
        x = x + g2 * (jnp.square(jax.nn.relu(h @ w_ff1[l])) @ w_ff2[l])
    return rmsnorm(x, final_g)
```

```python
import os
import numpy as np
from contextlib import ExitStack
import concourse.bass as bass
import concourse.mybir as mybir
from concourse.bass_utils import run_bass_kernel_spmd

F32 = mybir.dt.float32
BF16 = mybir.dt.bfloat16
AF = mybir.ActivationFunctionType
ALU = mybir.AluOpType

D = 1024
KC = 8
TT = 512
NTB = 4
INW = 3080
NPP = 129
RMS_EPS = 1e-6
LN_EPS = 1e-5
NEG = -30000.0

C_ID, C_NMI, C_NMS, C_MC, C_BO, C_H0, C_H1, C_CM, C_ONE = [i * 128 for i in range(9)]
NCONST = 9 * 128


class Prog:
    def __init__(self, nc, es):
        self.nc = nc
        self.es = es
        self.e = {"pe": nc.tensor, "act": nc.scalar, "dve": nc.vector, "pool": nc.gpsimd, "sp": nc.sync}
        self.sem = {k: es.enter_context(nc.semaphore("s_" + k)) for k in self.e}
        self.cnt = {k: 0 for k in self.e}
        self.seen = {k: {} for k in self.e}
        self.w = {}
        self.r = {}
        self.dsem = {}
        self.dcnt = {}
        self.prefix_ok = set()
        self.n_thr = 0
        self.n_ins = 0
        self.pending = []
        self.pool_ops = 0

    def _handle(self, sk):
        return self.sem[sk] if sk in self.sem else self.dsem[sk]

    def _wait(self, eng, sk, v):
        if sk == eng and eng == "pe":
            return
        if sk in self.dsem and sk not in self.prefix_ok:
            v = self.dcnt[sk]
        if self.seen[eng].get(sk, 0) >= v:
            return
        self.e[eng].wait_ge(self._handle(sk), v)
        self.seen[eng][sk] = v

    def _deps(self, eng, reads, writes):
        deps = {}
        for k in reads:
            w = self.w.get(k)
            if w is not None:
                deps[w[0]] = max(deps.get(w[0], 0), w[1])
        for k in writes:
            w = self.w.get(k)
            if w is not None:
                deps[w[0]] = max(deps.get(w[0], 0), w[1])
            for sk, v in self.r.get(k, {}).items():
                deps[sk] = max(deps.get(sk, 0), v)
        for sk, v in deps.items():
            self._wait(eng, sk, v)

    def _mark(self, me, reads, writes):
        for k in reads:
            d = self.r.setdefault(k, {})
            d[me[0]] = max(d.get(me[0], 0), me[1])
        for k in writes:
            self.w[k] = me
            self.r[k] = {}

    def op(self, eng, fn, reads=(), writes=(), inc=True):
        self._deps(eng, reads, writes)
        ins = fn(self.e[eng])
        self.n_ins += 1
        if inc:
            self.cnt[eng] += 1
            ins.then_inc(self.sem[eng], 1)
            me = (eng, self.cnt[eng])
        else:
            me = (eng, self.cnt[eng] + 1)
        self._mark(me, reads, writes)
        if eng == "pool" and self.pending:
            self.pool_ops += 1
            if self.pool_ops % 2 == 0:
                self.pending.pop(0)[1]()
        return ins

    def flush_pending(self, upto_tag):
        while self.pending and self.pending[0][0] <= upto_tag:
            self.pending.pop(0)[1]()

    def dma(self, eng, semname, out, in_, reads=(), writes=()):
        if semname not in self.dsem:
            self.dsem[semname] = self.es.enter_context(self.nc.semaphore("d_" + semname))
            self.dcnt[semname] = 0
        self._deps(eng, reads, writes)
        ins = self.e[eng].dma_start(out=out, in_=in_)
        ins.then_inc(self.dsem[semname], 16)
        self.dcnt[semname] += 16
        self.n_ins += 1
        me = (semname, self.dcnt[semname])
        self._mark(me, reads, writes)

    def dma_throttled(self, eng, out, in_, reads=(), writes=(), depth=2):
        k = f"thr{self.n_thr % depth}"
        self.n_thr += 1
        self.prefix_ok.add(k)
        if k in self.dsem and self.dcnt[k] > 0:
            self._wait(eng, k, self.dcnt[k])
        self.dma(eng, k, out, in_, reads, writes)

    def fence(self, engs, keys):
        for eng in engs:
            self._deps(eng, (), keys)

    def barrier(self):
        for eng in self.e:
            for k in self.e:
                if self.cnt[k] > 0:
                    self._wait(eng, k, self.cnt[k])
            for k in self.dsem:
                if self.dcnt[k] > 0:
                    self._wait(eng, k, self.dcnt[k])

    def final_wait(self, eng, semnames):
        for k in semnames:
            self._wait(eng, k, self.dcnt[k])


def build(L, T, do_final, dbg=None):
    NT = T // TT
    nc = bass.Bass("TRN2", target_bir_lowering=False)

    def din(name, shape, dt=F32):
        return nc.dram_tensor(name, list(shape), dt, kind="ExternalInput").ap()

    xT = din("xT", [D, T])
    cT = din("cT", [128, KC])
    w_ada = din("w_ada", [L, D, 6 * D])
    w_in = din("w_in", [L, D, INW])
    w_out = din("w_out", [L, D, D])
    w_ff1 = din("w_ff1", [L, D, 4 * D])
    w_ff2 = din("w_ff2", [L, 4 * D, D])
    pp = din("pp", [L, 128, NPP])
    pbs = din("pbs", [L, 128, 512])
    swT = din("swT", [L, 128, 512])
    fg = din("fg", [128, KC])
    consts = din("consts", [128, NCONST])
    lmask = din("lmask", [128, 6 * 256])
    outT = nc.dram_tensor("outT", [D, T], F32, kind="ExternalOutput").ap()
    scr_in = nc.dram_tensor("scr_in", [L, 7, 128, 4096], BF16, kind="Internal").ap()
    scr_out = nc.dram_tensor("scr_out", [L, 2, 128, 4096], BF16, kind="Internal").ap()
    scr_f1 = nc.dram_tensor("scr_f1", [L, 8, 128, 4096], BF16, kind="Internal").ap()
    scr_f2 = nc.dram_tensor("scr_f2", [L, 8, 128, 4096], BF16, kind="Internal").ap()
    dbg_out = {}
    if dbg:
        for name, shape in dbg.items():
            dbg_out[name] = nc.dram_tensor("dbg_" + name, list(shape), F32, kind="ExternalOutput").ap()

    with ExitStack() as es:
        P = Prog(nc, es)

        def sb(name, shape, dt=F32):
            return es.enter_context(nc.sbuf_tensor(name, list(shape), dt))

        def ps(name, shape, dt=F32):
            return es.enter_context(nc.psum_tensor(name, list(shape), dt))

        def cast_jobs(l):
            jobs = []
            src = w_in[l].rearrange("(kc p) n -> p kc n", p=128)
            for g in range(6):
                jobs.append((scr_in[l, g].rearrange("p (kc n) -> p kc n", kc=KC), src[:, :, g * 512:(g + 1) * 512], f"scr_in{l}"))
            jobs.append((scr_in[l, 6][:, 0:64].rearrange("p (kc n) -> p kc n", kc=KC), src[:, :, 3072:3080], f"scr_in{l}"))
            src = w_out[l].rearrange("(kc p) n -> p kc n", p=128)
            for g in range(2):
                jobs.append((scr_out[l, g].rearrange("p (kc n) -> p kc n", kc=KC), src[:, :, g * 512:(g + 1) * 512], f"scr_out{l}"))
            src = w_ff1[l].rearrange("(kc p) n -> p kc n", p=128)
            for g in range(8):
                jobs.append((scr_f1[l, g].rearrange("p (kc n) -> p kc n", kc=KC), src[:, :, g * 512:(g + 1) * 512], f"scr_f1{l}"))
            src = w_ff2[l].rearrange("(kc p) n -> p kc n", p=128)
            for g in range(8):
                jobs.append((scr_f2[l, g].rearrange("p (kc n) -> p kc n", kc=32), src[:, :, g * 128:(g + 1) * 128], f"scr_f2{l}"))
            return jobs

        for l in range(L):
            for (dst, src_, key) in cast_jobs(l):
                if l == 0:
                    P.dma_throttled("pool", dst, src_, writes=[key])
                else:
                    P.pending.append((l, lambda dst=dst, src_=src_, key=key: P.dma_throttled("pool", dst, src_, writes=[key])))

        CONST = sb("CONST", [128, NCONST])
        FG = sb("FG", [128, KC])
        IDB = sb("IDB", [128, 128], BF16)
        IIB = sb("IIB", [128, 256], BF16)
        ONESB = sb("ONESB", [128, 128], BF16)
        PPS = sb("PPS", [128, L, NPP])
        MOD = sb("MOD", [128, L, 48])
        A1 = sb("A1", [128, L, KC])
        A2 = sb("A2", [128, L, KC])
        NEA16 = sb("NEA16", [128, L, 16])
        DTB16 = sb("DTB16", [128, L, 16])
        WMT = sb("WMT", [128, L, 512], BF16)
        MH = sb("MH", [128, L, 512])
        CA = sb("CA", [128, KC])
        S = sb("S", [128, L * 4, 128])
        SBF = sb("SBF", [128, L * 4, 128], BF16)
        CVT = sb("CVT", [128, L, 12, 4], BF16)
        LM = sb("LM", [128, 6, 256], BF16)

        IDF = CONST[:, C_ID:C_ID + 128]
        NMI = CONST[:, C_NMI:C_NMI + 128]
        NMS = CONST[:, C_NMS:C_NMS + 128]
        ONESF = CONST[:, C_ONE:C_ONE + 128]

        PB = [ps(f"PB{i}", [128, 512]) for i in range(6)]
        PT = [ps(f"PT{i}", [128, 1024], BF16) for i in range(2)]
        st = {"pb": 0, "pt": 0}

        def big():
            i = st["pb"]
            st["pb"] = (i + 1) % 6
            return PB[i], f"PB{i}"

        small = big

        def ptile():
            i = st["pt"]
            st["pt"] = (i + 1) % 2
            return PT[i], f"PTt{i}"

        P.dma("sp", "ld_c", CONST[:], consts, writes=["CONST"])
        P.dma("sp", "ld_c", FG[:], fg, writes=["FG"])
        P.dma("sp", "ld_c", CA[:], cT, writes=["CA"])
        for l in range(L):
            P.dma("sp", "ld_c", PPS[:, l, :], pp[l], writes=["PPS"])
        P.op("dve", lambda e: e.tensor_copy(out=IDB[:], in_=IDF), ["CONST"], ["IDB"])
        P.op("dve", lambda e: e.tensor_copy(out=IIB[:, 0:128], in_=IDF), ["CONST"], ["IIB"])
        P.op("dve", lambda e: e.tensor_copy(out=IIB[:, 128:256], in_=IDF), ["CONST"], ["IIB"])
        P.op("dve", lambda e: e.tensor_copy(out=ONESB[:], in_=ONESF), ["CONST"], ["ONESB"])
        P.op("dve", lambda e: e.memset(S[:], 0.0), [], ["S"])
        P.op("dve", lambda e: e.memset(SBF[:], 0.0), [], ["SBF"])
        P.op("dve", lambda e: e.memset(CVT[:], 0.0), [], ["CVT"])
        P.op("act", lambda e: e.activation(out=CA[:], in_=CA[:], func=AF.Silu), ["CA"], ["CA"])

        KSTAGE = int(os.environ.get('KSTAGE', '9'))
        with ExitStack() as es2:
            def sb2(name, shape, dt=F32):
                return es2.enter_context(nc.sbuf_tensor(name, list(shape), dt))
            WA = [sb2(f"WA{i}", [128, KC, 768]) for i in range(2)]
            SWT = sb2("SWT", [128, 512])
            WMF = sb2("WMF", [128, 512])
            BSB = sb2("BSB", [128, 512])
            TMP4 = sb2("TMP4", [128, 4])
            LMF = sb2("LMF", [128, 6 * 256])
            P.dma("sp", "ld_lm", LMF[:], lmask, writes=["LMF"])
            P.op("dve", lambda e: e.tensor_copy(out=LM[:].rearrange("p a b -> p (a b)"), in_=LMF[:]), ["LMF"], ["LM"])
            for l in range(L if KSTAGE >= 2 else 0):
                pm, pmk = big()
                wsrc = w_ada[l].rearrange("(kc p) n -> p kc n", p=128)
                for q in range(8):
                    wa = WA[q % 2]
                    wk = f"WA{q % 2}"
                    P.dma("sp", wk, wa[:], wsrc[:, :, q * 768:(q + 1) * 768], writes=[wk])
                    for jj in range(6):
                        j = q * 6 + jj
                        for kc in range(KC):
                            P.op("pe", lambda e, wa=wa, jj=jj, kc=kc, j=j: e.matmul(
                                pm[:, j:j + 1], lhsT=wa[:, kc, jj * 128:(jj + 1) * 128], rhs=CA[:, kc:kc + 1],
                                start=(kc == 0), stop=(kc == KC - 1)),
                                [wk, "CA"], [pmk], inc=(kc == KC - 1))
                P.op("dve", lambda e: e.tensor_tensor(out=MOD[:, l, :], in0=pm[:, 0:48], in1=PPS[:, l, 16:64], op=ALU.add),
                     [pmk, "PPS"], ["MOD"])
                P.op("dve", lambda e: e.scalar_tensor_tensor(out=A1[:, l, :], in0=MOD[:, l, 8:16], scalar=1.0,
                                                              in1=PPS[:, l, 0:8], op0=ALU.add, op1=ALU.mult),
                     ["MOD", "PPS"], ["A1"])
                P.op("dve", lambda e: e.scalar_tensor_tensor(out=A2[:, l, :], in0=MOD[:, l, 32:40], scalar=1.0,
                                                              in1=PPS[:, l, 8:16], op0=ALU.add, op1=ALU.mult),
                     ["MOD", "PPS"], ["A2"])
                P.op("act", lambda e: e.activation(out=TMP4[:], in_=PPS[:, l, 121:125], func=AF.Exp), ["PPS"], ["TMP4"])
                for tb in range(4):
                    P.op("dve", lambda e, tb=tb: e.tensor_scalar(out=NEA16[:, l, tb * 4:tb * 4 + 4], in0=TMP4[:], scalar1=-1.0,
                                                                 scalar2=None, op0=ALU.mult), ["TMP4"], ["NEA16"])
                    P.op("dve", lambda e, tb=tb: e.tensor_copy(out=DTB16[:, l, tb * 4:tb * 4 + 4], in_=PPS[:, l, 125:129]),
                         ["PPS"], ["DTB16"])
                P.dma("sp", "ld_sw", SWT[:], swT[l], writes=["SWT"])
                P.dma("sp", "ld_sw", BSB[:], pbs[l], writes=["BSB"])
                for h in range(4):
                    P.op("dve", lambda e, h=h: e.tensor_tensor(out=WMF[:, h * 128:(h + 1) * 128], in0=SWT[:, h * 128:(h + 1) * 128],
                                                               in1=CONST[:, C_CM:C_CM + 128], op=ALU.mult),
                         ["SWT", "CONST"], ["WMF"])
                P.op("dve", lambda e: e.tensor_copy(out=WMT[:, l, :], in_=WMF[:]), ["WMF"], ["WMT"])
                pr, prk = big()
                P.op("pe", lambda e: e.matmul(pr[:], lhsT=ONESF, rhs=WMF[:], start=True, stop=True), ["CONST", "WMF"], [prk])
                for h in range(4):
                    P.op("dve", lambda e, h=h: e.scalar_tensor_tensor(
                        out=MH[:, l, h * 128:(h + 1) * 128], in0=pr[:, h * 128:(h + 1) * 128], scalar=PPS[:, l, 68 + h:69 + h],
                        in1=BSB[:, h * 128:(h + 1) * 128], op0=ALU.mult, op1=ALU.add), [prk, "PPS", "BSB"], ["MH"])
            P.barrier()

        X = sb("X", [128, KC, TT])
        RSB = [sb(f"RS{i}", [128, TT]) for i in range(2)]
        LNTB = [sb(f"LNT{i}", [128, TT]) for i in range(2)]
        XN = [sb(f"XN{i}", [128, TT]) for i in range(2)]
        HT = sb("HT", [128, KC, TT], BF16)
        UG = sb("UG", [128, 4, TT], BF16)
        VSG = sb("VSG", [128, TT])
        NRM = sb("NRM", [128, NTB, 512], BF16)
        ZS = sb("ZS", [128, 4, TT], BF16)
        ARENA = sb("ARENA", [128, 32 * TT], BF16)
        H1 = ARENA[:].rearrange("p (c t) -> p c t", c=32)
        CV = ARENA[:, 0:12 * 516].rearrange("p (c t) -> p c t", c=12)
        QKS = ARENA[:, 6272:6272 + 4096].rearrange("p (c t) -> p c t", c=8)
        QKN = ARENA[:, 10368:10368 + 4096].rearrange("p (c t) -> p c t", c=8)
        VT = sb("VT", [128, 4, TT], BF16)
        KTM = sb("KTM", [128, NTB, 512], BF16)
        VTM = sb("VTM", [128, NTB, 512], BF16)
        MIXT = sb("MIXT", [128, KC, TT], BF16)
        SQ = MIXT
        OT = HT
        NR = 3
        RING = sb("RING", [128, NR, 4096], BF16)
        DG = sb("DG", [128, 48, 128], BF16)
        BA4 = sb("BA4", [128, NTB, 8])
        T16 = [sb(f"T16_{i}", [128, NTB, 4]) for i in range(3)]
        BETA = sb("BETA", [128, NTB, 4])
        LNB = sb("LNB", [128, NTB, 4])
        G16 = sb("G16", [128, NTB, 4])
        GC4 = sb("GC4", [128, NTB, 16])
        GAM = sb("GAM", [128, NTB, 4])
        BGAM = sb("BGAM", [128, NTB, 4])
        KDS = sb("KDS", [128, NTB, 4])
        GLE = sb("GLE", [128, NTB, 8])
        ST6 = sb("ST6", [128, 6])
        MV = sb("MV", [128, 2])
        RSTD = sb("RSTD", [128, 1])
        NSET = 4
        gt = []
        for i in range(NSET):
            gt.append(dict(
                MG=sb(f"MG{i}", [128, 128]), MG2=sb(f"MG2{i}", [128, 128]),
                E1=sb(f"E1{i}", [128, 128]), E2=sb(f"E2{i}", [128, 128]),
                QKT=sb(f"QKT{i}", [128, 128], BF16),
                UN=[sb(f"UN{i}_{j}", [128, 256], BF16) for j in range(2)],
                PP=[sb(f"PP{i}_{j}", [128, 256], BF16) for j in range(2)],
                BV=sb(f"BV{i}", [128, 128], BF16), BGK=sb(f"BGK{i}", [128, 128], BF16),
                KDEC=sb(f"KDEC{i}", [128, 128], BF16), UNEWB=sb(f"UNEWB{i}", [128, 128], BF16),
                WKT=sb(f"WKT{i}", [128, 128], BF16), GB=sb(f"GB{i}", [128, 128]),
                QDT=sb(f"QDT{i}", [128, 128], BF16), WSB=sb(f"WSB{i}", [128, 128], BF16),
            ))

        if os.environ.get("KSB"):
            print("SBUF remaining after alloc:", nc.sbuf_bytes_remaining)
        GDN_KEYS = ["CV", "QKS", "QKN"]
        H1_KEYS = [f"H1_{j}" for j in range(32)]

        seq = []
        for ti in range(NT):
            for l in range(L):
                for g in (2, 3, 4, 6, 1, 0, 5):
                    seq.append(("in", l, g))
                for g in range(2):
                    seq.append(("out", l, g))
                for g in range(8):
                    seq.append(("f1", l, g))
                for g in range(8):
                    seq.append(("f2", l, g))
        scr = {"in": scr_in, "out": scr_out, "f1": scr_f1, "f2": scr_f2}
        wst = {"issued": 0, "used": 0}

        def issue_loads(upto):
            while wst["issued"] < min(upto, len(seq)):
                n = wst["issued"]
                kind, l, g = seq[n]
                P.flush_pending(l)
                slot = n % NR
                if kind == "in" and g == 6:
                    P.dma("sp", f"ring{slot}", RING[:, slot, 0:64], scr[kind][l, g][:, 0:64],
                          reads=[f"scr_{kind}{l}"], writes=[f"RING{slot}"])
                else:
                    P.dma("sp", f"ring{slot}", RING[:, slot, :], scr[kind][l, g],
                          reads=[f"scr_{kind}{l}"], writes=[f"RING{slot}"])
                wst["issued"] += 1

        def next_piece(kind, l, g):
            n = wst["used"]
            assert seq[n] == (kind, l, g), (seq[n], kind, l, g)
            issue_loads(n + NR)
            wst["used"] += 1
            slot = n % NR
            return RING[:, slot, :], f"RING{slot}"

        rst = {"i": 0}

        def rstd_bc(pt, ptk, scale, eps, extra_bias=0.0):
            i = rst["i"]
            rst["i"] = 1 - i
            lnt, lk, rs, rk = LNTB[i], f"LNT{i}", RSB[i], f"RS{i}"
            P.op("act", lambda e: e.activation(out=lnt[:], in_=pt[:], func=AF.Ln, scale=scale, bias=EPS[eps][:, 0:1]),
                 [ptk, "EPSC"], [lk])
            if extra_bias == 0.0:
                P.op("act", lambda e: e.activation(out=rs[:], in_=lnt[:], func=AF.Exp, scale=-0.5), [lk], [rk])
            else:
                P.op("act", lambda e: e.activation(out=rs[:], in_=lnt[:], func=AF.Exp, scale=-0.5,
                                                   bias=EPS["qs"][:, 0:1]), [lk, "EPSC"], [rk])
            return rs, rk

        EPSC = sb("EPSC", [128, 4])
        EPS = {"rms": EPSC[:, 0:1], "ln": EPSC[:, 1:2], "one": EPSC[:, 2:3], "qs": EPSC[:, 3:4]}
        P.op("dve", lambda e: e.memset(EPSC[:, 0:1], RMS_EPS), [], ["EPSC"])
        P.op("dve", lambda e: e.memset(EPSC[:, 1:2], LN_EPS), [], ["EPSC"])
        for _ in range(1 + int(os.environ.get("KNONCE", "0"))):
            P.op("dve", lambda e: e.memset(EPSC[:, 2:3], 1.0), [], ["EPSC"])
        P.op("dve", lambda e: e.memset(EPSC[:, 3:4], float(-0.5 * np.log(128.0))), [], ["EPSC"])

        def norm_mod(A, SH, l):
            for kc in range(KC):
                P.op("act", lambda e, kc=kc: e.activation(out=SQ[:, kc, :], in_=X[:, kc, :], func=AF.Square),
                     [f"X{kc}"], [f"MIXT{kc}"])
            pt, ptk = big()
            for kc in range(KC):
                P.op("pe", lambda e, kc=kc: e.matmul(pt[:], lhsT=ONESB[:], rhs=SQ[:, kc, :], start=(kc == 0), stop=(kc == KC - 1)),
                     ["ONESB", f"MIXT{kc}"], [ptk], inc=(kc == KC - 1))
            rs, rk = rstd_bc(pt, ptk, 1.0 / D, "rms")
            for kc in range(KC):
                xn = XN[kc % 2]
                xk = f"XN{kc % 2}"
                P.op("dve", lambda e, kc=kc, xn=xn: e.scalar_tensor_tensor(out=xn[:], in0=X[:, kc, :], scalar=A[:, l, kc:kc + 1],
                                                                            in1=rs[:], op0=ALU.mult, op1=ALU.mult),
                     [f"X{kc}", rk, "A"], [xk])
                P.op("act", lambda e, kc=kc, xn=xn: e.activation(out=HT[:, kc, :], in_=xn[:], func=AF.Identity, bias=SH(l, kc), scale=1.0),
                     [xk, "MOD"], [f"HT{kc}"])

        def gemm_fm(wt, wk, nk, c, rhs_fn, rhs_keys, kstride):
            pt, ptk = big()
            for kc in range(nk):
                P.op("pe", lambda e, kc=kc: e.matmul(pt[:], lhsT=wt[:, kc * kstride + c * 128: kc * kstride + (c + 1) * 128],
                                                     rhs=rhs_fn(kc), start=(kc == 0), stop=(kc == nk - 1)),
                     [wk] + rhs_keys, [ptk], inc=(kc == nk - 1))
            return pt, ptk

        HTK = [f"HT{kc}" for kc in range(KC)]

        def tile_layer(l, ti):
            first = (ti == 0)
            P.fence(["act", "dve", "pool"], H1_KEYS)
            SH1 = lambda l, kc: MOD[:, l, 0 + kc:1 + kc]
            SH2 = lambda l, kc: MOD[:, l, 24 + kc:25 + kc]
            norm_mod(A1, SH1, l)
            if KSTAGE == 3:
                return
            P.op("pool", lambda e: e.tensor_copy(out=CV[:, :, 0:3], in_=CVT[:, l, :, 0:3]), ["CVT"], ["CV"])
            for j in range(4):
                for c in range(12):
                    cwj = PPS[:, l, 72 + j * 12 + c:73 + j * 12 + c]
                    if c % 2 == 0:
                        P.op("act", lambda e, j=j, c=c, cwj=cwj: e.activation(out=DG[:, j * 12 + c, :], in_=IDB[:], func=AF.Copy, scale=cwj),
                             ["IDB", "PPS"], ["DG"])
                    else:
                        P.op("pool", lambda e, j=j, c=c, cwj=cwj: e.tensor_scalar(out=DG[:, j * 12 + c, :], in0=IDB[:], scalar1=cwj,
                                                                                  scalar2=0.0, op0=ALU.mult, op1=ALU.add), ["IDB", "PPS"], ["DG"])
            n_ev = 0
            for g in (2, 3, 4):
                wt, wk = next_piece("in", l, g)
                for c in range(4):
                    pt, ptk = gemm_fm(wt, wk, KC, c, lambda kc: HT[:, kc, :], HTK, 512)
                    cc = (g - 2) * 4 + c
                    if n_ev % 2 == 0:
                        P.op("act", lambda e, cc=cc, pt=pt: e.activation(out=CV[:, cc, 3:515], in_=pt[:], func=AF.Copy), [ptk], ["CV"])
                    else:
                        P.op("dve", lambda e, cc=cc, pt=pt: e.tensor_copy(out=CV[:, cc, 3:515], in_=pt[:]), [ptk], ["CV"])
                    n_ev += 1
            wt, wk = next_piece("in", l, 6)
            for tb in range(NTB):
                pt, ptk = small()
                for kc in range(KC):
                    P.op("pe", lambda e, kc=kc, tb=tb, pt=pt: e.matmul(pt[:, 0:8], lhsT=HT[:, kc, tb * 128:(tb + 1) * 128],
                                                                       rhs=wt[:, kc * 8:(kc + 1) * 8], start=(kc == 0), stop=(kc == KC - 1)),
                         [wk] + HTK, [ptk], inc=(kc == KC - 1))
                P.op("dve", lambda e, tb=tb, pt=pt: e.tensor_copy(out=BA4[:, tb, :], in_=pt[:, 0:8]), [ptk], ["BA4"])
            wt, wk = next_piece("in", l, 1)
            for tb in range(NTB):
                pt, ptk = big()
                for kc in range(KC):
                    P.op("pe", lambda e, kc=kc, tb=tb, pt=pt: e.matmul(pt[:], lhsT=HT[:, kc, tb * 128:(tb + 1) * 128],
                                                                       rhs=wt[:, kc * 512:(kc + 1) * 512], start=(kc == 0), stop=(kc == KC - 1)),
                         [wk] + HTK, [ptk], inc=(kc == KC - 1))
                P.op("act", lambda e, pt=pt: e.activation(out=VSG[:], in_=pt[:], func=AF.Gelu), [ptk], ["VSG"])
                P.op("dve", lambda e: e.bn_stats(out=ST6[:], in_=VSG[:]), ["VSG"], ["ST6"])
                P.op("dve", lambda e: e.bn_aggr(out=MV[:], in_=ST6[:]), ["ST6"], ["MV"])
                P.op("act", lambda e: e.activation(out=RSTD[:], in_=MV[:, 1:2], func=AF.Ln, bias=EPS["ln"][:, 0:1]), ["MV", "EPSC"], ["RSTD"])
                P.op("act", lambda e: e.activation(out=RSTD[:], in_=RSTD[:], func=AF.Exp, scale=-0.5), ["RSTD"], ["RSTD"])
                P.op("dve", lambda e, tb=tb: e.tensor_scalar(out=NRM[:, tb, :], in0=VSG[:], scalar1=MV[:, 0:1], scalar2=RSTD[:, 0:1],
                                                             op0=ALU.subtract, op1=ALU.mult), ["VSG", "MV", "RSTD"], [f"NRM{tb}"])
            wt, wk = next_piece("in", l, 0)
            for c in range(4):
                pt, ptk = gemm_fm(wt, wk, KC, c, lambda kc: HT[:, kc, :], HTK, 512)
                P.op("act", lambda e, c=c, pt=pt: e.activation(out=UG[:, c, :], in_=pt[:], func=AF.Gelu), [ptk], [f"UG{c}"])
            wt, wk = next_piece("in", l, 5)
            for c in range(4):
                pt, ptk = gemm_fm(wt, wk, KC, c, lambda kc: HT[:, kc, :], HTK, 512)
                P.op("act", lambda e, c=c, pt=pt: e.activation(out=ZS[:, c, :], in_=pt[:], func=AF.Silu), [ptk], [f"ZS{c}"])

            if KSTAGE == 4:
                return
            bra = BA4[:, :, 0:4]
            ara = BA4[:, :, 4:8]
            P.op("act", lambda e: e.activation(out=T16[0][:], in_=bra, func=AF.Exp, scale=-1.0), ["BA4"], ["T16_0"])
            P.op("act", lambda e: e.activation(out=T16[0][:], in_=T16[0][:], func=AF.Ln, bias=EPS["one"][:, 0:1]), ["T16_0", "EPSC"], ["T16_0"])
            P.op("act", lambda e: e.activation(out=BETA[:], in_=T16[0][:], func=AF.Exp, scale=-1.0), ["T16_0"], ["BETA"])
            P.op("dve", lambda e: e.tensor_scalar(out=LNB[:], in0=T16[0][:], scalar1=-1.0, scalar2=None, op0=ALU.mult), ["T16_0"], ["LNB"])
            P.op("dve", lambda e: e.tensor_tensor(out=T16[1][:], in0=ara, in1=DTB16[:, l, :].rearrange("p (a b) -> p a b", a=4), op=ALU.add),
                 ["BA4", "DTB16"], ["T16_1"])
            P.op("act", lambda e: e.activation(out=T16[1][:], in_=T16[1][:], func=AF.Exp), ["T16_1"], ["T16_1"])
            P.op("act", lambda e: e.activation(out=T16[1][:], in_=T16[1][:], func=AF.Ln, bias=EPS["one"][:, 0:1]), ["T16_1", "EPSC"], ["T16_1"])
            P.op("dve", lambda e: e.tensor_tensor(out=G16[:], in0=T16[1][:], in1=NEA16[:, l, :].rearrange("p (a b) -> p a b", a=4), op=ALU.mult),
                 ["T16_1", "NEA16"], ["G16"])
            for tb in range(NTB):
                pt, ptk = small()
                for q, off in enumerate((C_MC, C_BO, C_H0, C_H1)):
                    P.op("pe", lambda e, q=q, off=off, tb=tb, pt=pt: e.matmul(pt[:, q * 4:q * 4 + 4], lhsT=CONST[:, off:off + 128],
                                                                              rhs=G16[:, tb, :], start=True, stop=True),
                         ["CONST", "G16"], [ptk], inc=(q == 3))
                P.op("dve", lambda e, tb=tb, pt=pt: e.tensor_copy(out=GC4[:, tb, :], in_=pt[:, 0:16]), [ptk], ["GC4"])
            gc_v = GC4[:, :, 0:4]
            P.op("act", lambda e: e.activation(out=GAM[:], in_=gc_v, func=AF.Exp), ["GC4"], ["GAM"])
            P.op("dve", lambda e: e.tensor_tensor(out=BGAM[:], in0=BETA[:], in1=GAM[:], op=ALU.mult), ["BETA", "GAM"], ["BGAM"])
            P.op("dve", lambda e: e.tensor_tensor(out=T16[2][:], in0=GC4[:, :, 4:8], in1=gc_v, op=ALU.subtract), ["GC4"], ["T16_2"])
            P.op("act", lambda e: e.activation(out=KDS[:], in_=T16[2][:], func=AF.Exp), ["T16_2"], ["KDS"])
            P.op("act", lambda e: e.activation(out=GLE[:], in_=GC4[:, :, 8:16], func=AF.Exp), ["GC4"], ["GLE"])

            if KSTAGE == 5:
                return
            for c in range(12):
                pt, ptk = big()
                for j in range(4):
                    P.op("pe", lambda e, c=c, j=j, pt=pt: e.matmul(pt[:], lhsT=DG[:, j * 12 + c, :], rhs=CV[:, c, j:j + 512],
                                                                   start=(j == 0), stop=(j == 3)), ["DG", "CV"], [ptk], inc=(j == 3))
                if c < 8:
                    P.op("act", lambda e, c=c, pt=pt: e.activation(out=QKS[:, c, :], in_=pt[:], func=AF.Silu), [ptk], ["QKS"])
                else:
                    P.op("act", lambda e, c=c, pt=pt: e.activation(out=VT[:, c - 8, :], in_=pt[:], func=AF.Silu), [ptk], ["VT"])
            if KSTAGE == 51:
                return
            P.op("pool", lambda e: e.tensor_copy(out=CVT[:, l, :, 0:3], in_=CV[:, :, 512:515]), ["CV"], ["CVT"])
            if KSTAGE == 52:
                return
            for c in range(8):
                P.op("act", lambda e, c=c: e.activation(out=SQ[:, c, :], in_=QKS[:, c, :], func=AF.Square), ["QKS"], [f"MIXT{c}"])
                pt, ptk = big()
                P.op("pe", lambda e, c=c, pt=pt: e.matmul(pt[:], lhsT=ONESB[:], rhs=SQ[:, c, :], start=True, stop=True),
                     ["ONESB", f"MIXT{c}"], [ptk])
                rs, rk = rstd_bc(pt, ptk, 1.0, "rms", extra_bias=(1.0 if c < 4 else 0.0))
                P.op("dve", lambda e, c=c, rs=rs: e.tensor_tensor(out=QKN[:, c, :], in0=QKS[:, c, :], in1=rs[:], op=ALU.mult),
                     ["QKS", rk], ["QKN"])
            if KSTAGE == 53:
                return
            for tb in range(NTB):
                tbs = slice(tb * 128, (tb + 1) * 128)
                pt, ptk = ptile()
                for h in range(4):
                    P.op("pe", lambda e, h=h, pt=pt, tbs=tbs: e.transpose(out=pt[:, h * 128:(h + 1) * 128], in_=QKN[:, 4 + h, tbs], identity=IDB[:]),
                         ["QKN", "IDB"], [ptk], inc=(h == 3))
                P.op("act", lambda e, tb=tb, pt=pt: e.activation(out=KTM[:, tb, :], in_=pt[:, 0:512], func=AF.Copy), [ptk], ["KTM"])
                pt, ptk = ptile()
                for h in range(4):
                    P.op("pe", lambda e, h=h, pt=pt, tbs=tbs: e.transpose(out=pt[:, h * 128:(h + 1) * 128], in_=VT[:, h, tbs], identity=IDB[:]),
                         ["VT", "IDB"], [ptk], inc=(h == 3))
                P.op("dve", lambda e, tb=tb, pt=pt: e.tensor_copy(out=VTM[:, tb, :], in_=pt[:, 0:512]), [ptk], ["VTM"])

            if KSTAGE == 6:
                return
            for tb in range(NTB):
                tbs = slice(tb * 128, (tb + 1) * 128)
                for h in range(4):
                    hs = slice(h * 128, (h + 1) * 128)
                    pt, ptk = small()
                    P.op("pe", lambda e, pt=pt, tb=tb, hs=hs: e.matmul(pt[:, 0:128], lhsT=NRM[:, tb, hs], rhs=WMT[:, l, hs], start=True, stop=True),
                         [f"NRM{tb}", "WMT"], [ptk])
                    xn = XN[h % 2]
                    xk = f"XN{h % 2}"
                    P.op("dve", lambda e, pt=pt, h=h, hs=hs, xn=xn: e.scalar_tensor_tensor(out=xn[:, 0:128], in0=pt[:, 0:128], scalar=PPS[:, l, 64 + h:65 + h],
                                                                                            in1=MH[:, l, hs], op0=ALU.mult, op1=ALU.add),
                         [ptk, "PPS", "MH"], [xk])
                    P.op("pool", lambda e, h=h, tbs=tbs, xn=xn: e.tensor_tensor(out=MIXT[:, h, tbs], in0=xn[:, 0:128], in1=UG[:, h, tbs], op=ALU.mult),
                         [xk, f"UG{h}"], [f"MIXT{h}"])

            if KSTAGE == 7:
                return
            def gdn_chain(tb, h):
                g = gt[h]
                sfx = f"_{h}"
                tbs = slice(tb * 128, (tb + 1) * 128)
                hs = slice(h * 128, (h + 1) * 128)
                kT = QKN[:, 4 + h, tbs]
                qT = QKN[:, h, tbs]
                sc = lambda t: t[:, tb, h:h + 1]
                un = g["UN"][0]
                P.op("pool", lambda e: e.tensor_scalar(out=g["MG"][:], in0=CONST[:, C_MC:C_MC + 128], scalar1=sc(G16), scalar2=0.0, op0=ALU.mult, op1=ALU.add),
                     ["CONST", "G16"], ["MG" + sfx])
                P.op("dve", lambda e: e.scalar_tensor_tensor(out=g["MG2"][:], in0=IDF, scalar=sc(LNB), in1=g["MG"][:], op0=ALU.mult, op1=ALU.add),
                     ["CONST", "LNB", "MG" + sfx], ["MG2" + sfx])
                P.op("act", lambda e: e.activation(out=g["BV"][:], in_=VTM[:, tb, hs], func=AF.Copy, scale=sc(BETA)),
                     ["VTM", "BETA"], ["BV" + sfx])
                P.op("act", lambda e: e.activation(out=g["BGK"][:], in_=KTM[:, tb, hs], func=AF.Copy, scale=sc(BGAM)),
                     ["KTM", "BGAM"], ["BGK" + sfx])
                P.op("pool", lambda e: e.tensor_scalar(out=g["KDEC"][:], in0=KTM[:, tb, hs], scalar1=sc(KDS), scalar2=0.0, op0=ALU.mult, op1=ALU.add),
                     ["KTM", "KDS"], ["KDEC" + sfx])
                yield
                pa, pak = small()
                pk, pkk = pa[:, 256:512], pak
                P.op("pe", lambda e: e.matmul(pa[:, 0:128], lhsT=ONESF, rhs=g["MG"][:], start=True, stop=True), ["CONST", "MG" + sfx], [pak], inc=False)
                P.op("pe", lambda e: e.matmul(pa[:, 128:256], lhsT=ONESF, rhs=g["MG2"][:], start=True, stop=True), ["CONST", "MG2" + sfx], [pak], inc=False)
                P.op("pe", lambda e: e.matmul(pk[:, 0:128], lhsT=kT, rhs=kT, start=True, stop=True), ["QKN"], [pkk], inc=False)
                P.op("pe", lambda e: e.matmul(pk[:, 128:256], lhsT=kT, rhs=qT, start=True, stop=True), ["QKN"], [pkk])
                yield
                gcs = GC4[:, tb, h:h + 1]
                P.op("dve", lambda e: e.scalar_tensor_tensor(out=g["E1"][:], in0=pa[:, 0:128], scalar=gcs, in1=NMI, op0=ALU.subtract, op1=ALU.min),
                     [pak, "GC4", "CONST"], ["E1" + sfx])
                P.op("dve", lambda e: e.scalar_tensor_tensor(out=g["E2"][:], in0=pa[:, 128:256], scalar=gcs, in1=NMS, op0=ALU.subtract, op1=ALU.min),
                     [pak, "GC4", "CONST"], ["E2" + sfx])
                P.op("dve", lambda e: e.tensor_copy(out=g["GB"][:], in_=pa[:, 0:128]), [pak], ["GB" + sfx])
                yield
                P.op("act", lambda e: e.activation(out=g["E1"][:], in_=g["E1"][:], func=AF.Exp), ["E1" + sfx], ["E1" + sfx])
                P.op("act", lambda e: e.activation(out=g["E2"][:], in_=g["E2"][:], func=AF.Exp), ["E2" + sfx], ["E2" + sfx])
                P.op("act", lambda e: e.activation(out=g["GB"][:], in_=g["GB"][:], func=AF.Exp), ["GB" + sfx], ["GB" + sfx])
                yield
                P.op("dve", lambda e: e.scalar_tensor_tensor(out=un[:, 0:128], in0=pk[:, 0:128], scalar=-1.0, in1=g["E2"][:], op0=ALU.mult, op1=ALU.mult),
                     [pkk, "E2" + sfx], ["UN0" + sfx])
                P.op("dve", lambda e: e.tensor_tensor(out=g["QKT"][:], in0=pk[:, 128:256], in1=g["E1"][:], op=ALU.mult), [pkk, "E1" + sfx], ["QKT" + sfx])
                P.op("pool", lambda e: e.tensor_tensor(out=g["QDT"][:], in0=qT, in1=g["GB"][:], op=ALU.mult), ["QKN", "GB" + sfx], ["QDT" + sfx])
                yield
                ptt, pttk = small()
                P.op("pe", lambda e: e.matmul(ptt[:, 0:128], lhsT=un[:, 0:128], rhs=IDB[:], start=True, stop=True), ["UN0" + sfx, "IDB"], [pttk])
                yield
                P.op("act", lambda e: e.activation(out=un[:, 128:256], in_=ptt[:, 0:128], func=AF.Copy), [pttk], ["UN0" + sfx])
                yield
                ppc = g["PP"][0]
                y0 = g["UN"][1]
                P.op("pool", lambda e: e.tensor_tensor(out=y0[:], in0=un[:], in1=LM[:, 0, :], op=ALU.mult), ["UN0" + sfx, "LM"], ["UN1" + sfx])
                P.op("pool", lambda e: e.tensor_tensor(out=ppc[:], in0=y0[:], in1=IIB[:], op=ALU.add), ["UN1" + sfx, "IIB"], ["PP0" + sfx])
                yield
                cur = 0
                for j in range(1, 6):
                    last = (j == 5)
                    tv, tvk = g["PP"][cur], f"PP{cur}" + sfx
                    tn, tnk = g["PP"][1 - cur], f"PP{1 - cur}" + sfx
                    yb, ybk = g["UN"][1], "UN1" + sfx
                    w_ = 128 if last else 256
                    p1, p1k = small()
                    P.op("pe", lambda e: e.matmul(p1[:, 0:128], lhsT=un[:, 128:256], rhs=tv[:, 0:128], start=True, stop=True),
                         ["UN0" + sfx, tvk], [p1k], inc=last)
                    if not last:
                        P.op("pe", lambda e: e.matmul(p1[:, 128:256], lhsT=un[:, 0:128], rhs=tv[:, 128:256], start=True, stop=True),
                             ["UN0" + sfx, tvk], [p1k])
                    yield
                    P.op("dve", lambda e: e.tensor_tensor(out=yb[:, 0:w_], in0=p1[:, 0:w_], in1=LM[:, j, 0:w_], op=ALU.mult), [p1k, "LM"], [ybk])
                    yield
                    p2, p2k = small()
                    P.op("pe", lambda e: e.matmul(p2[:, 0:128], lhsT=tv[:, 128:256], rhs=yb[:, 0:128], start=True, stop=False),
                         [tvk, ybk], [p2k], inc=False)
                    P.op("pe", lambda e: e.matmul(p2[:, 0:128], lhsT=IDB[:], rhs=tv[:, 0:128], start=False, stop=True),
                         [tvk, "IDB"], [p2k], inc=last)
                    if not last:
                        P.op("pe", lambda e: e.matmul(p2[:, 128:256], lhsT=tv[:, 0:128], rhs=yb[:, 128:256], start=True, stop=False),
                             [tvk, ybk], [p2k], inc=False)
                        P.op("pe", lambda e: e.matmul(p2[:, 128:256], lhsT=IDB[:], rhs=tv[:, 128:256], start=False, stop=True),
                             [tvk, "IDB"], [p2k])
                    yield
                    P.op("act", lambda e: e.activation(out=tn[:, 0:w_], in_=p2[:, 0:w_], func=AF.Copy), [p2k], [tnk])
                    yield
                    cur = 1 - cur
                pu, puk = g["PP"][cur], f"PP{cur}" + sfx
                psu, psuk = small()
                P.op("pe", lambda e: e.matmul(psu[:, 0:128], lhsT=pu[:, 0:128], rhs=g["BV"][:], start=True, stop=True), [puk, "BV" + sfx], [psuk], inc=False)
                P.op("pe", lambda e: e.matmul(psu[:, 128:256], lhsT=g["BGK"][:], rhs=pu[:, 0:128], start=True, stop=True), [puk, "BGK" + sfx], [psuk])
                yield
                P.op("act", lambda e: e.activation(out=g["UNEWB"][:], in_=psu[:, 0:128], func=AF.Copy), [psuk], ["UNEW" + sfx])
                P.op("act", lambda e: e.activation(out=g["WKT"][:], in_=psu[:, 128:256], func=AF.Copy, scale=-1.0), [psuk], ["WKT" + sfx])
                yield
                sidx = l * 4 + h
                skey = f"S{sidx}"
                sbk = f"SBF{sidx}"
                for c in range(2):
                    R = slice(c * 64, (c + 1) * 64)
                    p1, p1k = small()
                    P.op("pe", lambda e: e.matmul(p1[R, 0:128], lhsT=g["WKT"][:, R], rhs=SBF[:, sidx, :], start=True, stop=False),
                         ["WKT" + sfx, sbk], [p1k], inc=False)
                    P.op("pe", lambda e: e.matmul(p1[R, 0:128], lhsT=IDB[R, R], rhs=g["UNEWB"][R, :], start=False, stop=True),
                         ["UNEW" + sfx, "IDB"], [p1k])
                    yield
                    P.op("act", lambda e: e.activation(out=g["WSB"][R, :], in_=p1[R, 0:128], func=AF.Copy), [p1k], ["WSB" + sfx])
                    yield
                    p2, p2k = small()
                    P.op("pe", lambda e: e.matmul(p2[:, 0:64], lhsT=SBF[:, sidx, :], rhs=g["QDT"][:, R], start=True, stop=False),
                         [sbk, "QDT" + sfx], [p2k], inc=False)
                    P.op("pe", lambda e: e.matmul(p2[:, 0:64], lhsT=g["WSB"][R, :], rhs=g["QKT"][R, R], start=False, stop=True),
                         ["WSB" + sfx, "QKT" + sfx], [p2k])
                    yield
                    P.op("act", lambda e: e.activation(out=OT[:, h, tb * 128 + c * 64: tb * 128 + (c + 1) * 64], in_=p2[:, 0:64], func=AF.Copy),
                         [p2k], [f"HT{h}"])
                    p3, p3k = small()
                    P.op("pe", lambda e: e.matmul(p3[:, 0:128], lhsT=g["KDEC"][R, :], rhs=g["WSB"][R, :], start=True, stop=True),
                         ["KDEC" + sfx, "WSB" + sfx], [p3k])
                    yield
                    P.op("dve", lambda e: e.scalar_tensor_tensor(out=S[:, sidx, :], in0=S[:, sidx, :], scalar=GLE[:, tb, c * 4 + h:c * 4 + h + 1],
                                                                  in1=p3[:, 0:128], op0=ALU.mult, op1=ALU.add),
                         [skey, "GLE", p3k], [skey])
                    yield
                    P.op("pool", lambda e: e.tensor_copy(out=SBF[:, sidx, :], in_=S[:, sidx, :]), [skey], [sbk])
                    yield

            for tb in range(NTB):
                alive = [gdn_chain(tb, h) for h in range(4)]
                while alive:
                    nxt = []
                    for gn in alive:
                        try:
                            next(gn)
                            nxt.append(gn)
                        except StopIteration:
                            pass
                    alive = nxt

            if KSTAGE == 8:
                return
            for h in range(4):
                P.op("act", lambda e, h=h: e.activation(out=SQ[:, 4 + h, :], in_=OT[:, h, :], func=AF.Square), [f"HT{h}"], [f"MIXT{4 + h}"])
                pt, ptk = big()
                P.op("pe", lambda e, h=h, pt=pt: e.matmul(pt[:], lhsT=ONESB[:], rhs=SQ[:, 4 + h, :], start=True, stop=True), ["ONESB", f"MIXT{4 + h}"], [ptk])
                rs, rk = rstd_bc(pt, ptk, 1.0 / 128, "rms")
                xn = XN[h % 2]
                xk = f"XN{h % 2}"
                P.op("dve", lambda e, h=h, xn=xn, rs=rs: e.scalar_tensor_tensor(out=xn[:], in0=OT[:, h, :], scalar=PPS[:, l, 120:121], in1=rs[:],
                                                                                 op0=ALU.mult, op1=ALU.mult), [f"HT{h}", "PPS", rk], [xk])
                P.op("pool", lambda e, h=h, xn=xn: e.tensor_tensor(out=MIXT[:, 4 + h, :], in0=xn[:], in1=ZS[:, h, :], op=ALU.mult),
                     [xk, f"ZS{h}"], [f"MIXT{4 + h}"])

            MIXK = [f"MIXT{k}" for k in range(KC)]
            for gq in range(2):
                wt, wk = next_piece("out", l, gq)
                for c in range(4):
                    j = gq * 4 + c
                    pt, ptk = gemm_fm(wt, wk, KC, c, lambda kc: MIXT[:, kc, :], MIXK, 512)
                    P.op("dve", lambda e, j=j, pt=pt: e.scalar_tensor_tensor(out=X[:, j, :], in0=pt[:], scalar=MOD[:, l, 16 + j:17 + j], in1=X[:, j, :],
                                                                              op0=ALU.mult, op1=ALU.add), [ptk, "MOD", f"X{j}"], [f"X{j}"])
            norm_mod(A2, SH2, l)
            P.fence(["act", "pool"], GDN_KEYS)
            for gq in range(8):
                wt, wk = next_piece("f1", l, gq)
                for c in range(4):
                    j = gq * 4 + c
                    pt, ptk = gemm_fm(wt, wk, KC, c, lambda kc: HT[:, kc, :], HTK, 512)
                    P.op("act", lambda e, pt=pt, j=j: e.activation(out=H1[:, j, :], in_=pt[:], func=AF.Relu), [ptk], [f"H1_{j}"])
                    if j % 2 == 0:
                        P.op("act", lambda e, j=j: e.activation(out=H1[:, j, :], in_=H1[:, j, :], func=AF.Square), [f"H1_{j}"], [f"H1_{j}"])
                    else:
                        P.op("pool", lambda e, j=j: e.tensor_tensor(out=H1[:, j, :], in0=H1[:, j, :], in1=H1[:, j, :], op=ALU.mult), [f"H1_{j}"], [f"H1_{j}"])
            for gq in range(8):
                wt, wk = next_piece("f2", l, gq)
                pt, ptk = big()
                for kc in range(32):
                    P.op("pe", lambda e, kc=kc, pt=pt: e.matmul(pt[:], lhsT=wt[:, kc * 128:(kc + 1) * 128], rhs=H1[:, kc, :],
                                                                start=(kc == 0), stop=(kc == 31)), [wk, f"H1_{kc}"], [ptk], inc=(kc == 31))
                P.op("dve", lambda e, gq=gq, pt=pt: e.scalar_tensor_tensor(out=X[:, gq, :], in0=pt[:], scalar=MOD[:, l, 40 + gq:41 + gq], in1=X[:, gq, :],
                                                                            op0=ALU.mult, op1=ALU.add), [ptk, "MOD", f"X{gq}"], [f"X{gq}"])

        XK = [f"X{kc}" for kc in range(KC)]
        xsrc = xT.rearrange("(kc p) t -> p kc t", p=128)
        odst = outT.rearrange("(kc p) t -> p kc t", p=128)
        for ti in range(NT):
            ts_ = slice(ti * TT, (ti + 1) * TT)
            P.dma("sp", "ld_x", X[:], xsrc[:, :, ts_], writes=XK)
            for l in range(L if KSTAGE >= 3 else 0):
                tile_layer(l, ti)
            if do_final:
                for kc in range(KC):
                    P.op("act", lambda e, kc=kc: e.activation(out=SQ[:, kc, :], in_=X[:, kc, :], func=AF.Square), [f"X{kc}"], [f"MIXT{kc}"])
                pt, ptk = big()
                for kc in range(KC):
                    P.op("pe", lambda e, kc=kc, pt=pt: e.matmul(pt[:], lhsT=ONESB[:], rhs=SQ[:, kc, :], start=(kc == 0), stop=(kc == KC - 1)),
                         ["ONESB", f"MIXT{kc}"], [ptk], inc=(kc == KC - 1))
                rs, rk = rstd_bc(pt, ptk, 1.0 / D, "rms")
                for kc in range(KC):
                    P.op("dve", lambda e, kc=kc, rs=rs: e.scalar_tensor_tensor(out=X[:, kc, :], in0=X[:, kc, :], scalar=FG[:, kc:kc + 1], in1=rs[:],
                                                                                op0=ALU.mult, op1=ALU.mult), [f"X{kc}", "FG", rk], [f"X{kc}"])
            P.dma("sp", "st_x", odst[:, :, ts_], X[:], reads=XK, writes=["OUT"])
        P.final_wait("sp", ["st_x"])
        if dbg:
            pass
    return nc


def _consts():
    s = np.arange(128)[:, None]
    t = np.arange(128)[None, :]
    same = (s // 64) == (t // 64)
    c = np.zeros((128, NCONST), np.float32)
    c[:, C_ID:C_ID + 128] = np.eye(128, dtype=np.float32)
    c[:, C_NMI:C_NMI + 128] = np.where(same & (s <= t), 0.0, NEG)
    c[:, C_NMS:C_NMS + 128] = np.where(same & (s < t), 0.0, NEG)
    c[:, C_MC:C_MC + 128] = np.where(same & (s <= t), 1.0, 0.0)
    c[:, C_BO:C_BO + 128] = np.where(same, 1.0, 0.0)
    c[:, C_H0:C_H0 + 128] = np.where(s < 64, 1.0, 0.0) + 0.0 * t
    c[:, C_H1:C_H1 + 128] = np.where(s >= 64, 1.0, 0.0) + 0.0 * t
    c[:, C_CM:C_CM + 128] = np.where(s <= t, 1.0, 0.0)
    c[:, C_ONE:C_ONE + 128] = 1.0
    return c


def _lmask():
    t = np.arange(128)[:, None]
    s_ = np.arange(128)[None, :]
    out = np.zeros((128, 6, 256), np.float32)
    for j in range(6):
        b = 2 ** j
        ml = ((t // (2 * b) == s_ // (2 * b)) & (t % (2 * b) >= b) & (s_ % (2 * b) < b)).astype(np.float32)
        out[:, j, 0:128] = ml.T
        out[:, j, 128:256] = ml
    return np.ascontiguousarray(out.reshape(128, 6 * 256))


def _pack_layer(l, norm1_g, norm2_g, b_ada, sgu_ln_g, sgu_ln_b, conv_w, gdn_norm_g, a_log, dt_bias, sgu_b, sgu_w):
    pp = np.zeros((128, NPP), np.float32)
    pp[:, 0:8] = norm1_g[l].reshape(8, 128).T
    pp[:, 8:16] = norm2_g[l].reshape(8, 128).T
    pp[:, 16:64] = b_ada[l].reshape(48, 128).T
    pp[:, 64:68] = sgu_ln_g[l].reshape(4, 128).T
    pp[:, 68:72] = sgu_ln_b[l].reshape(4, 128).T
    for j in range(4):
        pp[:, 72 + j * 12:72 + (j + 1) * 12] = conv_w[l, j].reshape(12, 128).T
    pp[:, 120] = gdn_norm_g[l]
    pp[:, 121:125] = np.broadcast_to(a_log[l][None, :], (128, 4))
    pp[:, 125:129] = np.broadcast_to(dt_bias[l][None, :], (128, 4))
    pbs = np.ascontiguousarray(np.broadcast_to(sgu_b[l].reshape(1, 512), (128, 512))).astype(np.float32)
    swT = np.ascontiguousarray(np.transpose(sgu_w[l], (2, 0, 1)).reshape(128, 512)).astype(np.float32)
    return pp, pbs, swT


_CACHE = {}


def _get_prog(L, T, do_final):
    key = (L, T, do_final)
    if key not in _CACHE:
        _CACHE[key] = build(L, T, do_final)
    return _CACHE[key]


FUSED = True


def kernel(x, c, w_ada, b_ada, norm1_g, w_in, sgu_ln_g, sgu_ln_b, sgu_w, sgu_b, conv_w,
           a_log, dt_bias, gdn_norm_g, w_out, norm2_g, w_ff1, w_ff2, final_g):
    args = [np.asarray(a, dtype=np.float32) for a in (x, c, w_ada, b_ada, norm1_g, w_in, sgu_ln_g, sgu_ln_b, sgu_w, sgu_b,
                                                       conv_w, a_log, dt_bias, gdn_norm_g, w_out, norm2_g, w_ff1, w_ff2, final_g)]
    (x, c, w_ada, b_ada, norm1_g, w_in, sgu_ln_g, sgu_ln_b, sgu_w, sgu_b, conv_w,
     a_log, dt_bias, gdn_norm_g, w_out, norm2_g, w_ff1, w_ff2, final_g) = args
    B, T, _ = x.shape
    depth = w_in.shape[0]
    consts = _consts()
    fgp = np.ascontiguousarray(final_g.reshape(8, 128).T)
    packs = [_pack_layer(l, norm1_g, norm2_g, b_ada, sgu_ln_g, sgu_ln_b, conv_w, gdn_norm_g, a_log, dt_bias, sgu_b, sgu_w)
             for l in range(depth)]
    xTs = [np.ascontiguousarray(x[b].T) for b in range(B)]
    cTs = [np.ascontiguousarray(c[b].reshape(8, 128).T) for b in range(B)]

    def launch(layers, xin, do_final):
        L = len(layers)
        nc = _get_prog(L, T, do_final)
        shared = {
            "w_ada": np.ascontiguousarray(w_ada[layers]), "w_in": np.ascontiguousarray(w_in[layers]),
            "w_out": np.ascontiguousarray(w_out[layers]), "w_ff1": np.ascontiguousarray(w_ff1[layers]),
            "w_ff2": np.ascontiguousarray(w_ff2[layers]),
            "pp": np.stack([packs[l][0] for l in layers]), "pbs": np.stack([packs[l][1] for l in layers]),
            "swT": np.stack([packs[l][2] for l in layers]), "fg": fgp, "consts": consts, "lmask": _lmask(),
        }
        in_maps = [dict(shared, xT=xin[b], cT=cTs[b]) for b in range(B)]
        res = run_bass_kernel_spmd(nc, in_maps, core_ids=list(range(B)))
        return [np.asarray(r["outT"]) for r in res.results]

    if FUSED:
        cur = launch(list(range(depth)), xTs, True)
    else:
        cur = xTs
        for l in range(depth):
            cur = launch([l], cur, l == depth - 1)
    out = np.stack([o.T for o in cur]).astype(np.float32)
    return out
```

```python
import os
import numpy as np
from contextlib import ExitStack
import concourse.bass as bass
import concourse.mybir as mybir
from concourse.bass_utils import run_bass_kernel_spmd

F32 = mybir.dt.float32
BF16 = mybir.dt.bfloat16
AF = mybir.ActivationFunctionType
ALU = mybir.AluOpType

D = 1024
KC = 8
TT = 512
NTB = 4
INW = 3080
NPP = 129
RMS_EPS = 1e-6
LN_EPS = 1e-5
NEG = -30000.0

C_ID, C_NMI, C_NMS, C_MC, C_BO, C_H0, C_H1, C_CM, C_ONE = [i * 128 for i in range(9)]
NCONST = 9 * 128


class Prog:
    def __init__(self, nc, es):
        self.nc = nc
        self.es = es
        self.e = {"pe": nc.tensor, "act": nc.scalar, "dve": nc.vector, "pool": nc.gpsimd, "sp": nc.sync}
        self.sem = {k: es.enter_context(nc.semaphore("s_" + k)) for k in self.e}
        self.cnt = {k: 0 for k in self.e}
        self.seen = {k: {} for k in self.e}
        self.w = {}
        self.r = {}
        self.dsem = {}
        self.dcnt = {}
        self.prefix_ok = set()
        self.n_thr = 0
        self.n_ins = 0
        self.pending = []
        self.pool_ops = 0

    def _handle(self, sk):
        return self.sem[sk] if sk in self.sem else self.dsem[sk]

    def _wait(self, eng, sk, v):
        if sk == eng and eng == "pe":
            return
        if sk in self.dsem and sk not in self.prefix_ok:
            v = self.dcnt[sk]
        if self.seen[eng].get(sk, 0) >= v:
            return
        self.e[eng].wait_ge(self._handle(sk), v)
        self.seen[eng][sk] = v

    def _deps(self, eng, reads, writes):
        deps = {}
        for k in reads:
            w = self.w.get(k)
            if w is not None:
                deps[w[0]] = max(deps.get(w[0], 0), w[1])
        for k in writes:
            w = self.w.get(k)
            if w is not None:
                deps[w[0]] = max(deps.get(w[0], 0), w[1])
            for sk, v in self.r.get(k, {}).items():
                deps[sk] = max(deps.get(sk, 0), v)
        for sk, v in deps.items():
            self._wait(eng, sk, v)

    def _mark(self, me, reads, writes):
        for k in reads:
            d = self.r.setdefault(k, {})
            d[me[0]] = max(d.get(me[0], 0), me[1])
        for k in writes:
            self.w[k] = me
            self.r[k] = {}

    def op(self, eng, fn, reads=(), writes=(), inc=True):
        self._deps(eng, reads, writes)
        ins = fn(self.e[eng])
        self.n_ins += 1
        if inc:
            self.cnt[eng] += 1
            ins.then_inc(self.sem[eng], 1)
            me = (eng, self.cnt[eng])
        else:
            me = (eng, self.cnt[eng] + 1)
        self._mark(me, reads, writes)
        if eng == "pool" and self.pending:
            self.pool_ops += 1
            if self.pool_ops % 2 == 0:
                self.pending.pop(0)[1]()
        return ins

    def flush_pending(self, upto_tag):
        while self.pending and self.pending[0][0] <= upto_tag:
            self.pending.pop(0)[1]()

    def dma(self, eng, semname, out, in_, reads=(), writes=()):
        if semname not in self.dsem:
            self.dsem[semname] = self.es.enter_context(self.nc.semaphore("d_" + semname))
            self.dcnt[semname] = 0
        self._deps(eng, reads, writes)
        ins = self.e[eng].dma_start(out=out, in_=in_)
        ins.then_inc(self.dsem[semname], 16)
        self.dcnt[semname] += 16
        self.n_ins += 1
        me = (semname, self.dcnt[semname])
        self._mark(me, reads, writes)

    def dma_throttled(self, eng, out, in_, reads=(), writes=(), depth=2):
        k = f"thr{self.n_thr % depth}"
        self.n_thr += 1
        self.prefix_ok.add(k)
        if k in self.dsem and self.dcnt[k] > 0:
            self._wait(eng, k, self.dcnt[k])
        self.dma(eng, k, out, in_, reads, writes)

    def fence(self, engs, keys):
        for eng in engs:
            self._deps(eng, (), keys)

    def barrier(self):
        for eng in self.e:
            for k in self.e:
                if self.cnt[k] > 0:
                    self._wait(eng, k, self.cnt[k])
            for k in self.dsem:
                if self.dcnt[k] > 0:
                    self._wait(eng, k, self.dcnt[k])

    def final_wait(self, eng, semnames):
        for k in semnames:
            self._wait(eng, k, self.dcnt[k])


def build(L, T, do_final, dbg=None):
    NT = T // TT
    nc = bass.Bass("TRN2", target_bir_lowering=False)

    def din(name, shape, dt=F32):
        return nc.dram_tensor(name, list(shape), dt, kind="ExternalInput").ap()

    xT = din("xT", [D, T])
    cT = din("cT", [128, KC])
    w_ada = din("w_ada", [L, D, 6 * D])
    w_in = din("w_in", [L, D, INW])
    w_out = din("w_out", [L, D, D])
    w_ff1 = din("w_ff1", [L, D, 4 * D])
    w_ff2 = din("w_ff2", [L, 4 * D, D])
    pp = din("pp", [L, 128, NPP])
    pbs = din("pbs", [L, 128, 512])
    swT = din("swT", [L, 128, 512])
    fg = din("fg", [128, KC])
    consts = din("consts", [128, NCONST])
    lmask = din("lmask", [128, 6 * 256])
    outT = nc.dram_tensor("outT", [D, T], F32, kind="ExternalOutput").ap()
    scr_in = nc.dram_tensor("scr_in", [L, 7, 128, 4096], BF16, kind="Internal").ap()
    scr_out = nc.dram_tensor("scr_out", [L, 2, 128, 4096], BF16, kind="Internal").ap()
    scr_f1 = nc.dram_tensor("scr_f1", [L, 8, 128, 4096], BF16, kind="Internal").ap()
    scr_f2 = nc.dram_tensor("scr_f2", [L, 8, 128, 4096], BF16, kind="Internal").ap()
    dbg_out = {}
    if dbg:
        for name, shape in dbg.items():
            dbg_out[name] = nc.dram_tensor("dbg_" + name, list(shape), F32, kind="ExternalOutput").ap()

    with ExitStack() as es:
        P = Prog(nc, es)

        def sb(name, shape, dt=F32):
            return es.enter_context(nc.sbuf_tensor(name, list(shape), dt))

        def ps(name, shape, dt=F32):
            return es.enter_context(nc.psum_tensor(name, list(shape), dt))

        def cast_jobs(l):
            jobs = []
            src = w_in[l].rearrange("(kc p) n -> p kc n", p=128)
            for g in range(6):
                jobs.append((scr_in[l, g].rearrange("p (kc n) -> p kc n", kc=KC), src[:, :, g * 512:(g + 1) * 512], f"scr_in{l}"))
            jobs.append((scr_in[l, 6][:, 0:64].rearrange("p (kc n) -> p kc n", kc=KC), src[:, :, 3072:3080], f"scr_in{l}"))
            src = w_out[l].rearrange("(kc p) n -> p kc n", p=128)
            for g in range(2):
                jobs.append((scr_out[l, g].rearrange("p (kc n) -> p kc n", kc=KC), src[:, :, g * 512:(g + 1) * 512], f"scr_out{l}"))
            src = w_ff1[l].rearrange("(kc p) n -> p kc n", p=128)
            for g in range(8):
                jobs.append((scr_f1[l, g].rearrange("p (kc n) -> p kc n", kc=KC), src[:, :, g * 512:(g + 1) * 512], f"scr_f1{l}"))
            src = w_ff2[l].rearrange("(kc p) n -> p kc n", p=128)
            for g in range(8):
                jobs.append((scr_f2[l, g].rearrange("p (kc n) -> p kc n", kc=32), src[:, :, g * 128:(g + 1) * 128], f"scr_f2{l}"))
            return jobs

        for l in range(L):
            for (dst, src_, key) in cast_jobs(l):
                if l == 0:
                    P.dma_throttled("pool", dst, src_, writes=[key])
                else:
                    P.pending.append((l, lambda dst=dst, src_=src_, key=key: P.dma_throttled("pool", dst, src_, writes=[key])))

        CONST = sb("CONST", [128, NCONST])
        FG = sb("FG", [128, KC])
        IDB = sb("IDB", [128, 128], BF16)
        IIB = sb("IIB", [128, 256], BF16)
        ONESB = sb("ONESB", [128, 128], BF16)
        PPS = sb("PPS", [128, L, NPP])
        MOD = sb("MOD", [128, L, 48])
        A1 = sb("A1", [128, L, KC])
        A2 = sb("A2", [128, L, KC])
        NEA16 = sb("NEA16", [128, L, 16])
        DTB16 = sb("DTB16", [128, L, 16])
        WMT = sb("WMT", [128, L, 512], BF16)
        MH = sb("MH", [128, L, 512])
        CA = sb("CA", [128, KC])
        S = sb("S", [128, L * 4, 128])
        SBF = sb("SBF", [128, L * 4, 128], BF16)
        CVT = sb("CVT", [128, L, 12, 4], BF16)
        LM = sb("LM", [128, 6, 256], BF16)

        IDF = CONST[:, C_ID:C_ID + 128]
        NMI = CONST[:, C_NMI:C_NMI + 128]
        NMS = CONST[:, C_NMS:C_NMS + 128]
        ONESF = CONST[:, C_ONE:C_ONE + 128]

        PB = [ps(f"PB{i}", [128, 512]) for i in range(6)]
        PT = [ps(f"PT{i}", [128, 1024], BF16) for i in range(2)]
        st = {"pb": 0, "pt": 0}

        def big():
            i = st["pb"]
            st["pb"] = (i + 1) % 6
            return PB[i], f"PB{i}"

        small = big

        def ptile():
            i = st["pt"]
            st["pt"] = (i + 1) % 2
            return PT[i], f"PTt{i}"

        P.dma("sp", "ld_c", CONST[:], consts, writes=["CONST"])
        P.dma("sp", "ld_c", FG[:], fg, writes=["FG"])
        P.dma("sp", "ld_c", CA[:], cT, writes=["CA"])
        for l in range(L):
            P.dma("sp", "ld_c", PPS[:, l, :], pp[l], writes=["PPS"])
        P.op("dve", lambda e: e.tensor_copy(out=IDB[:], in_=IDF), ["CONST"], ["IDB"])
        P.op("dve", lambda e: e.tensor_copy(out=IIB[:, 0:128], in_=IDF), ["CONST"], ["IIB"])
        P.op("dve", lambda e: e.tensor_copy(out=IIB[:, 128:256], in_=IDF), ["CONST"], ["IIB"])
        P.op("dve", lambda e: e.tensor_copy(out=ONESB[:], in_=ONESF), ["CONST"], ["ONESB"])
        P.op("dve", lambda e: e.memset(S[:], 0.0), [], ["S"])
        P.op("dve", lambda e: e.memset(SBF[:], 0.0), [], ["SBF"])
        P.op("dve", lambda e: e.memset(CVT[:], 0.0), [], ["CVT"])
        P.op("act", lambda e: e.activation(out=CA[:], in_=CA[:], func=AF.Silu), ["CA"], ["CA"])

        KSTAGE = int(os.environ.get('KSTAGE', '9'))
        with ExitStack() as es2:
            def sb2(name, shape, dt=F32):
                return es2.enter_context(nc.sbuf_tensor(name, list(shape), dt))
            WA = [sb2(f"WA{i}", [128, KC, 768]) for i in range(2)]
            SWT = sb2("SWT", [128, 512])
            WMF = sb2("WMF", [128, 512])
            BSB = sb2("BSB", [128, 512])
            TMP4 = sb2("TMP4", [128, 4])
            LMF = sb2("LMF", [128, 6 * 256])
            P.dma("sp", "ld_lm", LMF[:], lmask, writes=["LMF"])
            P.op("dve", lambda e: e.tensor_copy(out=LM[:].rearrange("p a b -> p (a b)"), in_=LMF[:]), ["LMF"], ["LM"])
            for l in range(L if KSTAGE >= 2 else 0):
                pm, pmk = big()
                wsrc = w_ada[l].rearrange("(kc p) n -> p kc n", p=128)
                for q in range(8):
                    wa = WA[q % 2]
                    wk = f"WA{q % 2}"
                    P.dma("sp", wk, wa[:], wsrc[:, :, q * 768:(q + 1) * 768], writes=[wk])
                    for jj in range(6):
                        j = q * 6 + jj
                        for kc in range(KC):
                            P.op("pe", lambda e, wa=wa, jj=jj, kc=kc, j=j: e.matmul(
                                pm[:, j:j + 1], lhsT=wa[:, kc, jj * 128:(jj + 1) * 128], rhs=CA[:, kc:kc + 1],
                                start=(kc == 0), stop=(kc == KC - 1)),
                                [wk, "CA"], [pmk], inc=(kc == KC - 1))
                P.op("dve", lambda e: e.tensor_tensor(out=MOD[:, l, :], in0=pm[:, 0:48], in1=PPS[:, l, 16:64], op=ALU.add),
                     [pmk, "PPS"], ["MOD"])
                P.op("dve", lambda e: e.scalar_tensor_tensor(out=A1[:, l, :], in0=MOD[:, l, 8:16], scalar=1.0,
                                                              in1=PPS[:, l, 0:8], op0=ALU.add, op1=ALU.mult),
                     ["MOD", "PPS"], ["A1"])
                P.op("dve", lambda e: e.scalar_tensor_tensor(out=A2[:, l, :], in0=MOD[:, l, 32:40], scalar=1.0,
                                                              in1=PPS[:, l, 8:16], op0=ALU.add, op1=ALU.mult),
                     ["MOD", "PPS"], ["A2"])
                P.op("act", lambda e: e.activation(out=TMP4[:], in_=PPS[:, l, 121:125], func=AF.Exp), ["PPS"], ["TMP4"])
                for tb in range(4):
                    P.op("dve", lambda e, tb=tb: e.tensor_scalar(out=NEA16[:, l, tb * 4:tb * 4 + 4], in0=TMP4[:], scalar1=-1.0,
                                                                 scalar2=None, op0=ALU.mult), ["TMP4"], ["NEA16"])
                    P.op("dve", lambda e, tb=tb: e.tensor_copy(out=DTB16[:, l, tb * 4:tb * 4 + 4], in_=PPS[:, l, 125:129]),
                         ["PPS"], ["DTB16"])
                P.dma("sp", "ld_sw", SWT[:], swT[l], writes=["SWT"])
                P.dma("sp", "ld_sw", BSB[:], pbs[l], writes=["BSB"])
                for h in range(4):
                    P.op("dve", lambda e, h=h: e.tensor_tensor(out=WMF[:, h * 128:(h + 1) * 128], in0=SWT[:, h * 128:(h + 1) * 128],
                                                               in1=CONST[:, C_CM:C_CM + 128], op=ALU.mult),
                         ["SWT", "CONST"], ["WMF"])
                P.op("dve", lambda e: e.tensor_copy(out=WMT[:, l, :], in_=WMF[:]), ["WMF"], ["WMT"])
                pr, prk = big()
                P.op("pe", lambda e: e.matmul(pr[:], lhsT=ONESF, rhs=WMF[:], start=True, stop=True), ["CONST", "WMF"], [prk])
                for h in range(4):
                    P.op("dve", lambda e, h=h: e.scalar_tensor_tensor(
                        out=MH[:, l, h * 128:(h + 1) * 128], in0=pr[:, h * 128:(h + 1) * 128], scalar=PPS[:, l, 68 + h:69 + h],
                        in1=BSB[:, h * 128:(h + 1) * 128], op0=ALU.mult, op1=ALU.add), [prk, "PPS", "BSB"], ["MH"])
            P.barrier()

        X = sb("X", [128, KC, TT])
        RSB = [sb(f"RS{i}", [128, TT]) for i in range(2)]
        LNTB = [sb(f"LNT{i}", [128, TT]) for i in range(2)]
        XN = [sb(f"XN{i}", [128, TT]) for i in range(2)]
        HT = sb("HT", [128, KC, TT], BF16)
        UG = sb("UG", [128, 4, TT], BF16)
        VSG = sb("VSG", [128, TT])
        NRM = sb("NRM", [128, NTB, 512], BF16)
        ZS = sb("ZS", [128, 4, TT], BF16)
        ARENA = sb("ARENA", [128, 32 * TT], BF16)
        H1 = ARENA[:].rearrange("p (c t) -> p c t", c=32)
        CV = ARENA[:, 0:12 * 516].rearrange("p (c t) -> p c t", c=12)
        QKS = ARENA[:, 6272:6272 + 4096].rearrange("p (c t) -> p c t", c=8)
        QKN = ARENA[:, 10368:10368 + 4096].rearrange("p (c t) -> p c t", c=8)
        VT = sb("VT", [128, 4, TT], BF16)
        KTM = sb("KTM", [128, NTB, 512], BF16)
        VTM = sb("VTM", [128, NTB, 512], BF16)
        MIXT = sb("MIXT", [128, KC, TT], BF16)
        SQ = MIXT
        OT = HT
        NR = 3
        RING = sb("RING", [128, NR, 4096], BF16)
        DG = sb("DG", [128, 48, 128], BF16)
        BA4 = sb("BA4", [128, NTB, 8])
        T16 = [sb(f"T16_{i}", [128, NTB, 4]) for i in range(3)]
        BETA = sb("BETA", [128, NTB, 4])
        LNB = sb("LNB", [128, NTB, 4])
        G16 = sb("G16", [128, NTB, 4])
        GC4 = sb("GC4", [128, NTB, 16])
        GAM = sb("GAM", [128, NTB, 4])
        BGAM = sb("BGAM", [128, NTB, 4])
        KDS = sb("KDS", [128, NTB, 4])
        GLE = sb("GLE", [128, NTB, 8])
        ST6 = sb("ST6", [128, 6])
        MV = sb("MV", [128, 2])
        RSTD = sb("RSTD", [128, 1])
        NSET = 4
        gt = []
        for i in range(NSET):
            gt.append(dict(
                MG=sb(f"MG{i}", [128, 128]), MG2=sb(f"MG2{i}", [128, 128]),
                E1=sb(f"E1{i}", [128, 128]), E2=sb(f"E2{i}", [128, 128]),
                UN=[sb(f"UN{i}_{j}", [128, 256], BF16) for j in range(2)],
                PP=[sb(f"PP{i}_{j}", [128, 256], BF16) for j in range(2)],
                BV=sb(f"BV{i}", [128, 128], BF16), BGK=sb(f"BGK{i}", [128, 128], BF16),
                GB=sb(f"GB{i}", [128, 128]), WSB=sb(f"WSB{i}", [128, 128], BF16),
                QKT=[sb(f"QKT{i}_{j}", [128, 128], BF16) for j in range(2)],
                KDEC=[sb(f"KDEC{i}_{j}", [128, 128], BF16) for j in range(2)],
                UNEW=[sb(f"UNEW{i}_{j}", [128, 128], BF16) for j in range(2)],
                WKT=[sb(f"WKT{i}_{j}", [128, 128], BF16) for j in range(2)],
                QDT=[sb(f"QDT{i}_{j}", [128, 128], BF16) for j in range(2)],
            ))
        if os.environ.get("KSB"):
            print("SBUF remaining after alloc:", nc.sbuf_bytes_remaining)
        GDN_KEYS = ["CV", "QKS", "QKN"]
        H1_KEYS = [f"H1_{j}" for j in range(32)]

        seq = []
        for ti in range(NT):
            for l in range(L):
                for g in (2, 3, 4, 6, 1, 0, 5):
                    seq.append(("in", l, g))
                for g in range(2):
                    seq.append(("out", l, g))
                for g in range(8):
                    seq.append(("f1", l, g))
                for g in range(8):
                    seq.append(("f2", l, g))
        scr = {"in": scr_in, "out": scr_out, "f1": scr_f1, "f2": scr_f2}
        wst = {"issued": 0, "used": 0}

        def issue_loads(upto):
            while wst["issued"] < min(upto, len(seq)):
                n = wst["issued"]
                kind, l, g = seq[n]
                P.flush_pending(l)
                slot = n % NR
                if kind == "in" and g == 6:
                    P.dma("sp", f"ring{slot}", RING[:, slot, 0:64], scr[kind][l, g][:, 0:64],
                          reads=[f"scr_{kind}{l}"], writes=[f"RING{slot}"])
                else:
                    P.dma("sp", f"ring{slot}", RING[:, slot, :], scr[kind][l, g],
                          reads=[f"scr_{kind}{l}"], writes=[f"RING{slot}"])
                wst["issued"] += 1

        def next_piece(kind, l, g):
            n = wst["used"]
            assert seq[n] == (kind, l, g), (seq[n], kind, l, g)
            issue_loads(n + NR)
            wst["used"] += 1
            slot = n % NR
            return RING[:, slot, :], f"RING{slot}"

        rst = {"i": 0}

        def rstd_bc(pt, ptk, scale, eps, extra_bias=0.0):
            i = rst["i"]
            rst["i"] = 1 - i
            lnt, lk, rs, rk = LNTB[i], f"LNT{i}", RSB[i], f"RS{i}"
            P.op("act", lambda e: e.activation(out=lnt[:], in_=pt[:], func=AF.Ln, scale=scale, bias=EPS[eps][:, 0:1]),
                 [ptk, "EPSC"], [lk])
            if extra_bias == 0.0:
                P.op("act", lambda e: e.activation(out=rs[:], in_=lnt[:], func=AF.Exp, scale=-0.5), [lk], [rk])
            else:
                P.op("act", lambda e: e.activation(out=rs[:], in_=lnt[:], func=AF.Exp, scale=-0.5,
                                                   bias=EPS["qs"][:, 0:1]), [lk, "EPSC"], [rk])
            return rs, rk

        EPSC = sb("EPSC", [128, 4])
        EPS = {"rms": EPSC[:, 0:1], "ln": EPSC[:, 1:2], "one": EPSC[:, 2:3], "qs": EPSC[:, 3:4]}
        P.op("dve", lambda e: e.memset(EPSC[:, 0:1], RMS_EPS), [], ["EPSC"])
        P.op("dve", lambda e: e.memset(EPSC[:, 1:2], LN_EPS), [], ["EPSC"])
        for _ in range(1 + int(os.environ.get("KNONCE", "0"))):
            P.op("dve", lambda e: e.memset(EPSC[:, 2:3], 1.0), [], ["EPSC"])
        P.op("dve", lambda e: e.memset(EPSC[:, 3:4], float(-0.5 * np.log(128.0))), [], ["EPSC"])

        def norm_mod(A, SH, l):
            for kc in range(KC):
                P.op("act", lambda e, kc=kc: e.activation(out=SQ[:, kc, :], in_=X[:, kc, :], func=AF.Square),
                     [f"X{kc}"], [f"MIXT{kc}"])
            pt, ptk = big()
            for kc in range(KC):
                P.op("pe", lambda e, kc=kc: e.matmul(pt[:], lhsT=ONESB[:], rhs=SQ[:, kc, :], start=(kc == 0), stop=(kc == KC - 1)),
                     ["ONESB", f"MIXT{kc}"], [ptk], inc=(kc == KC - 1))
            rs, rk = rstd_bc(pt, ptk, 1.0 / D, "rms")
            for kc in range(KC):
                xn = XN[kc % 2]
                xk = f"XN{kc % 2}"
                P.op("dve", lambda e, kc=kc, xn=xn: e.scalar_tensor_tensor(out=xn[:], in0=X[:, kc, :], scalar=A[:, l, kc:kc + 1],
                                                                            in1=rs[:], op0=ALU.mult, op1=ALU.mult),
                     [f"X{kc}", rk, "A"], [xk])
                P.op("act", lambda e, kc=kc, xn=xn: e.activation(out=HT[:, kc, :], in_=xn[:], func=AF.Identity, bias=SH(l, kc), scale=1.0),
                     [xk, "MOD"], [f"HT{kc}"])

        def gemm_fm(wt, wk, nk, c, rhs_fn, rhs_keys, kstride):
            pt, ptk = big()
            for kc in range(nk):
                P.op("pe", lambda e, kc=kc: e.matmul(pt[:], lhsT=wt[:, kc * kstride + c * 128: kc * kstride + (c + 1) * 128],
                                                     rhs=rhs_fn(kc), start=(kc == 0), stop=(kc == nk - 1)),
                     [wk] + rhs_keys, [ptk], inc=(kc == nk - 1))
            return pt, ptk

        HTK = [f"HT{kc}" for kc in range(KC)]

        def tile_layer(l, ti):
            first = (ti == 0)
            P.fence(["act", "dve", "pool"], H1_KEYS)
            SH1 = lambda l, kc: MOD[:, l, 0 + kc:1 + kc]
            SH2 = lambda l, kc: MOD[:, l, 24 + kc:25 + kc]
            norm_mod(A1, SH1, l)
            if KSTAGE == 3:
                return
            P.op("pool", lambda e: e.tensor_copy(out=CV[:, :, 0:3], in_=CVT[:, l, :, 0:3]), ["CVT"], ["CV"])
            for j in range(4):
                for c in range(12):
                    cwj = PPS[:, l, 72 + j * 12 + c:73 + j * 12 + c]
                    if c % 2 == 0:
                        P.op("act", lambda e, j=j, c=c, cwj=cwj: e.activation(out=DG[:, j * 12 + c, :], in_=IDB[:], func=AF.Copy, scale=cwj),
                             ["IDB", "PPS"], ["DG"])
                    else:
                        P.op("pool", lambda e, j=j, c=c, cwj=cwj: e.tensor_scalar(out=DG[:, j * 12 + c, :], in0=IDB[:], scalar1=cwj,
                                                                                  scalar2=0.0, op0=ALU.mult, op1=ALU.add), ["IDB", "PPS"], ["DG"])
            n_ev = 0
            for g in (2, 3, 4):
                wt, wk = next_piece("in", l, g)
                for c in range(4):
                    pt, ptk = gemm_fm(wt, wk, KC, c, lambda kc: HT[:, kc, :], HTK, 512)
                    cc = (g - 2) * 4 + c
                    if n_ev % 2 == 0:
                        P.op("act", lambda e, cc=cc, pt=pt: e.activation(out=CV[:, cc, 3:515], in_=pt[:], func=AF.Copy), [ptk], ["CV"])
                    else:
                        P.op("dve", lambda e, cc=cc, pt=pt: e.tensor_copy(out=CV[:, cc, 3:515], in_=pt[:]), [ptk], ["CV"])
                    n_ev += 1
            wt, wk = next_piece("in", l, 6)
            for tb in range(NTB):
                pt, ptk = small()
                for kc in range(KC):
                    P.op("pe", lambda e, kc=kc, tb=tb, pt=pt: e.matmul(pt[:, 0:8], lhsT=HT[:, kc, tb * 128:(tb + 1) * 128],
                                                                       rhs=wt[:, kc * 8:(kc + 1) * 8], start=(kc == 0), stop=(kc == KC - 1)),
                         [wk] + HTK, [ptk], inc=(kc == KC - 1))
                P.op("dve", lambda e, tb=tb, pt=pt: e.tensor_copy(out=BA4[:, tb, :], in_=pt[:, 0:8]), [ptk], ["BA4"])
            wt, wk = next_piece("in", l, 1)
            for tb in range(NTB):
                pt, ptk = big()
                for kc in range(KC):
                    P.op("pe", lambda e, kc=kc, tb=tb, pt=pt: e.matmul(pt[:], lhsT=HT[:, kc, tb * 128:(tb + 1) * 128],
                                                                       rhs=wt[:, kc * 512:(kc + 1) * 512], start=(kc == 0), stop=(kc == KC - 1)),
                         [wk] + HTK, [ptk], inc=(kc == KC - 1))
                P.op("act", lambda e, pt=pt: e.activation(out=VSG[:], in_=pt[:], func=AF.Gelu), [ptk], ["VSG"])
                P.op("dve", lambda e: e.bn_stats(out=ST6[:], in_=VSG[:]), ["VSG"], ["ST6"])
                P.op("dve", lambda e: e.bn_aggr(out=MV[:], in_=ST6[:]), ["ST6"], ["MV"])
                P.op("act", lambda e: e.activation(out=RSTD[:], in_=MV[:, 1:2], func=AF.Ln, bias=EPS["ln"][:, 0:1]), ["MV", "EPSC"], ["RSTD"])
                P.op("act", lambda e: e.activation(out=RSTD[:], in_=RSTD[:], func=AF.Exp, scale=-0.5), ["RSTD"], ["RSTD"])
                P.op("dve", lambda e, tb=tb: e.tensor_scalar(out=NRM[:, tb, :], in0=VSG[:], scalar1=MV[:, 0:1], scalar2=RSTD[:, 0:1],
                                                             op0=ALU.subtract, op1=ALU.mult), ["VSG", "MV", "RSTD"], [f"NRM{tb}"])
            wt, wk = next_piece("in", l, 0)
            for c in range(4):
                pt, ptk = gemm_fm(wt, wk, KC, c, lambda kc: HT[:, kc, :], HTK, 512)
                P.op("act", lambda e, c=c, pt=pt: e.activation(out=UG[:, c, :], in_=pt[:], func=AF.Gelu), [ptk], [f"UG{c}"])
            wt, wk = next_piece("in", l, 5)
            for c in range(4):
                pt, ptk = gemm_fm(wt, wk, KC, c, lambda kc: HT[:, kc, :], HTK, 512)
                P.op("act", lambda e, c=c, pt=pt: e.activation(out=ZS[:, c, :], in_=pt[:], func=AF.Silu), [ptk], [f"ZS{c}"])

            if KSTAGE == 4:
                return
            bra = BA4[:, :, 0:4]
            ara = BA4[:, :, 4:8]
            P.op("act", lambda e: e.activation(out=T16[0][:], in_=bra, func=AF.Exp, scale=-1.0), ["BA4"], ["T16_0"])
            P.op("act", lambda e: e.activation(out=T16[0][:], in_=T16[0][:], func=AF.Ln, bias=EPS["one"][:, 0:1]), ["T16_0", "EPSC"], ["T16_0"])
            P.op("act", lambda e: e.activation(out=BETA[:], in_=T16[0][:], func=AF.Exp, scale=-1.0), ["T16_0"], ["BETA"])
            P.op("dve", lambda e: e.tensor_scalar(out=LNB[:], in0=T16[0][:], scalar1=-1.0, scalar2=None, op0=ALU.mult), ["T16_0"], ["LNB"])
            P.op("dve", lambda e: e.tensor_tensor(out=T16[1][:], in0=ara, in1=DTB16[:, l, :].rearrange("p (a b) -> p a b", a=4), op=ALU.add),
                 ["BA4", "DTB16"], ["T16_1"])
            P.op("act", lambda e: e.activation(out=T16[1][:], in_=T16[1][:], func=AF.Exp), ["T16_1"], ["T16_1"])
            P.op("act", lambda e: e.activation(out=T16[1][:], in_=T16[1][:], func=AF.Ln, bias=EPS["one"][:, 0:1]), ["T16_1", "EPSC"], ["T16_1"])
            P.op("dve", lambda e: e.tensor_tensor(out=G16[:], in0=T16[1][:], in1=NEA16[:, l, :].rearrange("p (a b) -> p a b", a=4), op=ALU.mult),
                 ["T16_1", "NEA16"], ["G16"])
            for tb in range(NTB):
                pt, ptk = small()
                for q, off in enumerate((C_MC, C_BO, C_H0, C_H1)):
                    P.op("pe", lambda e, q=q, off=off, tb=tb, pt=pt: e.matmul(pt[:, q * 4:q * 4 + 4], lhsT=CONST[:, off:off + 128],
                                                                              rhs=G16[:, tb, :], start=True, stop=True),
                         ["CONST", "G16"], [ptk], inc=(q == 3))
                P.op("dve", lambda e, tb=tb, pt=pt: e.tensor_copy(out=GC4[:, tb, :], in_=pt[:, 0:16]), [ptk], ["GC4"])
            gc_v = GC4[:, :, 0:4]
            P.op("act", lambda e: e.activation(out=GAM[:], in_=gc_v, func=AF.Exp), ["GC4"], ["GAM"])
            P.op("dve", lambda e: e.tensor_tensor(out=BGAM[:], in0=BETA[:], in1=GAM[:], op=ALU.mult), ["BETA", "GAM"], ["BGAM"])
            P.op("dve", lambda e: e.tensor_tensor(out=T16[2][:], in0=GC4[:, :, 4:8], in1=gc_v, op=ALU.subtract), ["GC4"], ["T16_2"])
            P.op("act", lambda e: e.activation(out=KDS[:], in_=T16[2][:], func=AF.Exp), ["T16_2"], ["KDS"])
            P.op("act", lambda e: e.activation(out=GLE[:], in_=GC4[:, :, 8:16], func=AF.Exp), ["GC4"], ["GLE"])

            if KSTAGE == 5:
                return
            for c in range(12):
                pt, ptk = big()
                for j in range(4):
                    P.op("pe", lambda e, c=c, j=j, pt=pt: e.matmul(pt[:], lhsT=DG[:, j * 12 + c, :], rhs=CV[:, c, j:j + 512],
                                                                   start=(j == 0), stop=(j == 3)), ["DG", "CV"], [ptk], inc=(j == 3))
                if c < 8:
                    P.op("act", lambda e, c=c, pt=pt: e.activation(out=QKS[:, c, :], in_=pt[:], func=AF.Silu), [ptk], ["QKS"])
                else:
                    P.op("act", lambda e, c=c, pt=pt: e.activation(out=VT[:, c - 8, :], in_=pt[:], func=AF.Silu), [ptk], ["VT"])
            if KSTAGE == 51:
                return
            P.op("pool", lambda e: e.tensor_copy(out=CVT[:, l, :, 0:3], in_=CV[:, :, 512:515]), ["CV"], ["CVT"])
            if KSTAGE == 52:
                return
            for c in range(8):
                P.op("act", lambda e, c=c: e.activation(out=SQ[:, c, :], in_=QKS[:, c, :], func=AF.Square), ["QKS"], [f"MIXT{c}"])
                pt, ptk = big()
                P.op("pe", lambda e, c=c, pt=pt: e.matmul(pt[:], lhsT=ONESB[:], rhs=SQ[:, c, :], start=True, stop=True),
                     ["ONESB", f"MIXT{c}"], [ptk])
                rs, rk = rstd_bc(pt, ptk, 1.0, "rms", extra_bias=(1.0 if c < 4 else 0.0))
                P.op("dve", lambda e, c=c, rs=rs: e.tensor_tensor(out=QKN[:, c, :], in0=QKS[:, c, :], in1=rs[:], op=ALU.mult),
                     ["QKS", rk], ["QKN"])
            if KSTAGE == 53:
                return
            for tb in range(NTB):
                tbs = slice(tb * 128, (tb + 1) * 128)
                pt, ptk = ptile()
                for h in range(4):
                    P.op("pe", lambda e, h=h, pt=pt, tbs=tbs: e.transpose(out=pt[:, h * 128:(h + 1) * 128], in_=QKN[:, 4 + h, tbs], identity=IDB[:]),
                         ["QKN", "IDB"], [ptk], inc=(h == 3))
                P.op("act", lambda e, tb=tb, pt=pt: e.activation(out=KTM[:, tb, :], in_=pt[:, 0:512], func=AF.Copy), [ptk], ["KTM"])
                pt, ptk = ptile()
                for h in range(4):
                    P.op("pe", lambda e, h=h, pt=pt, tbs=tbs: e.transpose(out=pt[:, h * 128:(h + 1) * 128], in_=VT[:, h, tbs], identity=IDB[:]),
                         ["VT", "IDB"], [ptk], inc=(h == 3))
                P.op("dve", lambda e, tb=tb, pt=pt: e.tensor_copy(out=VTM[:, tb, :], in_=pt[:, 0:512]), [ptk], ["VTM"])

            if KSTAGE == 6:
                return
            for tb in range(NTB):
                tbs = slice(tb * 128, (tb + 1) * 128)
                for h in range(4):
                    hs = slice(h * 128, (h + 1) * 128)
                    pt, ptk = small()
                    P.op("pe", lambda e, pt=pt, tb=tb, hs=hs: e.matmul(pt[:, 0:128], lhsT=NRM[:, tb, hs], rhs=WMT[:, l, hs], start=True, stop=True),
                         [f"NRM{tb}", "WMT"], [ptk])
                    xn = XN[h % 2]
                    xk = f"XN{h % 2}"
                    P.op("dve", lambda e, pt=pt, h=h, hs=hs, xn=xn: e.scalar_tensor_tensor(out=xn[:, 0:128], in0=pt[:, 0:128], scalar=PPS[:, l, 64 + h:65 + h],
                                                                                            in1=MH[:, l, hs], op0=ALU.mult, op1=ALU.add),
                         [ptk, "PPS", "MH"], [xk])
                    P.op("pool", lambda e, h=h, tbs=tbs, xn=xn: e.tensor_tensor(out=MIXT[:, h, tbs], in0=xn[:, 0:128], in1=UG[:, h, tbs], op=ALU.mult),
                         [xk, f"UG{h}"], [f"MIXT{h}"])

            if KSTAGE == 7:
                return
            def gdn_pre(tb, h):
                g = gt[h]
                par = tb % 2
                sfx = f"_{h}"
                psx = f"_{h}_{par}"
                tbs = slice(tb * 128, (tb + 1) * 128)
                hs = slice(h * 128, (h + 1) * 128)
                kT = QKN[:, 4 + h, tbs]
                qT = QKN[:, h, tbs]
                sc = lambda t: t[:, tb, h:h + 1]
                un = g["UN"][0]
                bank, bk = PB[h], f"PB{h}"
                QKT, KDEC, UNEW, WKT, QDT = g["QKT"][par], g["KDEC"][par], g["UNEW"][par], g["WKT"][par], g["QDT"][par]
                P.op("dve", lambda e: e.tensor_scalar(out=g["MG"][:], in0=CONST[:, C_MC:C_MC + 128], scalar1=sc(G16), scalar2=None, op0=ALU.mult),
                     ["CONST", "G16"], ["MG" + sfx])
                P.op("dve", lambda e: e.scalar_tensor_tensor(out=g["MG2"][:], in0=IDF, scalar=sc(LNB), in1=g["MG"][:], op0=ALU.mult, op1=ALU.add),
                     ["CONST", "LNB", "MG" + sfx], ["MG2" + sfx])
                P.op("act", lambda e: e.activation(out=g["BV"][:], in_=VTM[:, tb, hs], func=AF.Copy, scale=sc(BETA)),
                     ["VTM", "BETA"], ["BV" + sfx])
                P.op("act", lambda e: e.activation(out=g["BGK"][:], in_=KTM[:, tb, hs], func=AF.Copy, scale=sc(BGAM)),
                     ["KTM", "BGAM"], ["BGK" + sfx])
                P.op("pool", lambda e: e.tensor_scalar(out=KDEC[:], in0=KTM[:, tb, hs], scalar1=sc(KDS), scalar2=0.0, op0=ALU.mult, op1=ALU.add),
                     ["KTM", "KDS"], ["KDEC" + psx])
                yield
                pa, pk = bank[:, 0:256], bank[:, 256:512]
                P.op("pe", lambda e: e.matmul(pa[:, 0:128], lhsT=ONESF, rhs=g["MG"][:], start=True, stop=True), ["CONST", "MG" + sfx], [bk], inc=False)
                P.op("pe", lambda e: e.matmul(pa[:, 128:256], lhsT=ONESF, rhs=g["MG2"][:], start=True, stop=True), ["CONST", "MG2" + sfx], [bk], inc=False)
                P.op("pe", lambda e: e.matmul(pk[:, 0:128], lhsT=kT, rhs=kT, start=True, stop=True), ["QKN"], [bk], inc=False)
                P.op("pe", lambda e: e.matmul(pk[:, 128:256], lhsT=kT, rhs=qT, start=True, stop=True), ["QKN"], [bk])
                yield
                gcs = GC4[:, tb, h:h + 1]
                P.op("dve", lambda e: e.scalar_tensor_tensor(out=g["E1"][:], in0=pa[:, 0:128], scalar=gcs, in1=NMI, op0=ALU.subtract, op1=ALU.min),
                     [bk, "GC4", "CONST"], ["E1" + sfx])
                P.op("dve", lambda e: e.scalar_tensor_tensor(out=g["E2"][:], in0=pa[:, 128:256], scalar=gcs, in1=NMS, op0=ALU.subtract, op1=ALU.min),
                     [bk, "GC4", "CONST"], ["E2" + sfx])
                P.op("dve", lambda e: e.tensor_copy(out=g["GB"][:], in_=pa[:, 0:128]), [bk], ["GB" + sfx])
                yield
                P.op("act", lambda e: e.activation(out=g["E1"][:], in_=g["E1"][:], func=AF.Exp), ["E1" + sfx], ["E1" + sfx])
                P.op("act", lambda e: e.activation(out=g["E2"][:], in_=g["E2"][:], func=AF.Exp), ["E2" + sfx], ["E2" + sfx])
                P.op("act", lambda e: e.activation(out=g["GB"][:], in_=g["GB"][:], func=AF.Exp), ["GB" + sfx], ["GB" + sfx])
                yield
                P.op("dve", lambda e: e.scalar_tensor_tensor(out=un[:, 0:128], in0=pk[:, 0:128], scalar=-1.0, in1=g["E2"][:], op0=ALU.mult, op1=ALU.mult),
                     [bk, "E2" + sfx], ["UN0" + sfx])
                P.op("dve", lambda e: e.tensor_tensor(out=QKT[:], in0=pk[:, 128:256], in1=g["E1"][:], op=ALU.mult), [bk, "E1" + sfx], ["QKT" + psx])
                P.op("pool", lambda e: e.tensor_tensor(out=QDT[:], in0=qT, in1=g["GB"][:], op=ALU.mult), ["QKN", "GB" + sfx], ["QDT" + psx])
                yield
                P.op("pe", lambda e: e.matmul(bank[:, 0:128], lhsT=un[:, 0:128], rhs=IDB[:], start=True, stop=True), ["UN0" + sfx, "IDB"], [bk])
                yield
                P.op("act", lambda e: e.activation(out=un[:, 128:256], in_=bank[:, 0:128], func=AF.Copy), [bk], ["UN0" + sfx])
                yield
                ppc = g["PP"][0]
                y0 = g["UN"][1]
                P.op("pool", lambda e: e.tensor_tensor(out=y0[:], in0=un[:], in1=LM[:, 0, :], op=ALU.mult), ["UN0" + sfx, "LM"], ["UN1" + sfx])
                P.op("pool", lambda e: e.tensor_tensor(out=ppc[:], in0=y0[:], in1=IIB[:], op=ALU.add), ["UN1" + sfx, "IIB"], ["PP0" + sfx])
                yield
                cur = 0
                for j in range(1, 6):
                    last = (j == 5)
                    tv, tvk = g["PP"][cur], f"PP{cur}" + sfx
                    tn, tnk = g["PP"][1 - cur], f"PP{1 - cur}" + sfx
                    yb, ybk = g["UN"][1], "UN1" + sfx
                    w_ = 128 if last else 256
                    P.op("pe", lambda e: e.matmul(bank[:, 0:128], lhsT=un[:, 128:256], rhs=tv[:, 0:128], start=True, stop=True),
                         ["UN0" + sfx, tvk], [bk], inc=last)
                    if not last:
                        P.op("pe", lambda e: e.matmul(bank[:, 128:256], lhsT=un[:, 0:128], rhs=tv[:, 128:256], start=True, stop=True),
                             ["UN0" + sfx, tvk], [bk])
                    yield
                    P.op("dve", lambda e: e.tensor_tensor(out=yb[:, 0:w_], in0=bank[:, 0:w_], in1=LM[:, j, 0:w_], op=ALU.mult), [bk, "LM"], [ybk])
                    yield
                    P.op("pe", lambda e: e.matmul(bank[:, 256:384], lhsT=tv[:, 128:256], rhs=yb[:, 0:128], start=True, stop=True),
                         [tvk, ybk], [bk], inc=last)
                    if not last:
                        P.op("pe", lambda e: e.matmul(bank[:, 384:512], lhsT=tv[:, 0:128], rhs=yb[:, 128:256], start=True, stop=True),
                             [tvk, ybk], [bk])
                    yield
                    P.op("dve", lambda e: e.tensor_tensor(out=tn[:, 0:w_], in0=bank[:, 256:256 + w_], in1=tv[:, 0:w_], op=ALU.add), [bk, tvk], [tnk])
                    yield
                    cur = 1 - cur
                pu, puk = g["PP"][cur], f"PP{cur}" + sfx
                P.op("pe", lambda e: e.matmul(bank[:, 0:128], lhsT=pu[:, 0:128], rhs=g["BV"][:], start=True, stop=True), [puk, "BV" + sfx], [bk], inc=False)
                P.op("pe", lambda e: e.matmul(bank[:, 128:256], lhsT=g["BGK"][:], rhs=pu[:, 0:128], start=True, stop=True), [puk, "BGK" + sfx], [bk])
                yield
                P.op("act", lambda e: e.activation(out=UNEW[:], in_=bank[:, 0:128], func=AF.Copy), [bk], ["UNEW" + psx])
                P.op("act", lambda e: e.activation(out=WKT[:], in_=bank[:, 128:256], func=AF.Copy), [bk], ["WKT" + psx])
                yield

            def gdn_scan(tb, h):
                g = gt[h]
                par = tb % 2
                sfx = f"_{h}"
                psx = f"_{h}_{par}"
                QKT, KDEC, UNEW, WKT, QDT = g["QKT"][par], g["KDEC"][par], g["UNEW"][par], g["WKT"][par], g["QDT"][par]
                sidx = l * 4 + h
                skey = f"S{sidx}"
                sbk = f"SBF{sidx}"
                b4, b5 = PB[4], PB[5]
                for c in range(2):
                    R = slice(c * 64, (c + 1) * 64)
                    p1 = b4[:, h * 128:(h + 1) * 128]
                    P.op("pe", lambda e: e.matmul(p1[R, :], lhsT=WKT[:, R], rhs=SBF[:, sidx, :], start=True, stop=True),
                         ["WKT" + psx, sbk], ["PB4"])
                    yield
                    P.op("dve", lambda e: e.tensor_tensor(out=g["WSB"][R, :], in0=UNEW[R, :], in1=p1[R, :], op=ALU.subtract),
                         ["UNEW" + psx, "PB4"], ["WSB" + sfx])
                    yield
                    p2 = b5[:, h * 64:(h + 1) * 64]
                    P.op("pe", lambda e: e.matmul(p2, lhsT=SBF[:, sidx, :], rhs=QDT[:, R], start=True, stop=False),
                         [sbk, "QDT" + psx], ["PB5"], inc=False)
                    P.op("pe", lambda e: e.matmul(p2, lhsT=g["WSB"][R, :], rhs=QKT[R, R], start=False, stop=True),
                         ["WSB" + sfx, "QKT" + psx], ["PB5"])
                    yield
                    P.op("act", lambda e: e.activation(out=OT[:, h, tb * 128 + c * 64: tb * 128 + (c + 1) * 64], in_=p2, func=AF.Copy),
                         ["PB5"], [f"HT{h}"])
                    p3 = b4[:, h * 128:(h + 1) * 128]
                    P.op("pe", lambda e: e.matmul(p3, lhsT=KDEC[R, :], rhs=g["WSB"][R, :], start=True, stop=True),
                         ["KDEC" + psx, "WSB" + sfx], ["PB4"])
                    yield
                    P.op("dve", lambda e: e.scalar_tensor_tensor(out=S[:, sidx, :], in0=S[:, sidx, :], scalar=GLE[:, tb, c * 4 + h:c * 4 + h + 1],
                                                                  in1=p3, op0=ALU.mult, op1=ALU.add),
                         [skey, "GLE", "PB4"], [skey])
                    yield
                    P.op("pool", lambda e: e.tensor_copy(out=SBF[:, sidx, :], in_=S[:, sidx, :]), [skey], [sbk])
                    yield

            for rnd in range(NTB + 1):
                alive = []
                if rnd < NTB:
                    alive += [gdn_pre(rnd, h) for h in range(4)]
                if rnd >= 1:
                    alive += [gdn_scan(rnd - 1, h) for h in range(4)]
                while alive:
                    nxt = []
                    for gn in alive:
                        try:
                            next(gn)
                            nxt.append(gn)
                        except StopIteration:
                            pass
                    alive = nxt

            if KSTAGE == 8:
                return
            for h in range(4):
                P.op("act", lambda e, h=h: e.activation(out=SQ[:, 4 + h, :], in_=OT[:, h, :], func=AF.Square), [f"HT{h}"], [f"MIXT{4 + h}"])
                pt, ptk = big()
                P.op("pe", lambda e, h=h, pt=pt: e.matmul(pt[:], lhsT=ONESB[:], rhs=SQ[:, 4 + h, :], start=True, stop=True), ["ONESB", f"MIXT{4 + h}"], [ptk])
                rs, rk = rstd_bc(pt, ptk, 1.0 / 128, "rms")
                xn = XN[h % 2]
                xk = f"XN{h % 2}"
                P.op("dve", lambda e, h=h, xn=xn, rs=rs: e.scalar_tensor_tensor(out=xn[:], in0=OT[:, h, :], scalar=PPS[:, l, 120:121], in1=rs[:],
                                                                                 op0=ALU.mult, op1=ALU.mult), [f"HT{h}", "PPS", rk], [xk])
                P.op("pool", lambda e, h=h, xn=xn: e.tensor_tensor(out=MIXT[:, 4 + h, :], in0=xn[:], in1=ZS[:, h, :], op=ALU.mult),
                     [xk, f"ZS{h}"], [f"MIXT{4 + h}"])

            MIXK = [f"MIXT{k}" for k in range(KC)]
            for gq in range(2):
                wt, wk = next_piece("out", l, gq)
                for c in range(4):
                    j = gq * 4 + c
                    pt, ptk = gemm_fm(wt, wk, KC, c, lambda kc: MIXT[:, kc, :], MIXK, 512)
                    P.op("dve", lambda e, j=j, pt=pt: e.scalar_tensor_tensor(out=X[:, j, :], in0=pt[:], scalar=MOD[:, l, 16 + j:17 + j], in1=X[:, j, :],
                                                                              op0=ALU.mult, op1=ALU.add), [ptk, "MOD", f"X{j}"], [f"X{j}"])
            norm_mod(A2, SH2, l)
            P.fence(["act", "pool"], GDN_KEYS)
            for gq in range(8):
                wt, wk = next_piece("f1", l, gq)
                for c in range(4):
                    j = gq * 4 + c
                    pt, ptk = gemm_fm(wt, wk, KC, c, lambda kc: HT[:, kc, :], HTK, 512)
                    P.op("act", lambda e, pt=pt, j=j: e.activation(out=H1[:, j, :], in_=pt[:], func=AF.Relu), [ptk], [f"H1_{j}"])
                    if j % 2 == 0:
                        P.op("act", lambda e, j=j: e.activation(out=H1[:, j, :], in_=H1[:, j, :], func=AF.Square), [f"H1_{j}"], [f"H1_{j}"])
                    else:
                        P.op("pool", lambda e, j=j: e.tensor_tensor(out=H1[:, j, :], in0=H1[:, j, :], in1=H1[:, j, :], op=ALU.mult), [f"H1_{j}"], [f"H1_{j}"])
            for gq in range(8):
                wt, wk = next_piece("f2", l, gq)
                pt, ptk = big()
                for kc in range(32):
                    P.op("pe", lambda e, kc=kc, pt=pt: e.matmul(pt[:], lhsT=wt[:, kc * 128:(kc + 1) * 128], rhs=H1[:, kc, :],
                                                                start=(kc == 0), stop=(kc == 31)), [wk, f"H1_{kc}"], [ptk], inc=(kc == 31))
                P.op("dve", lambda e, gq=gq, pt=pt: e.scalar_tensor_tensor(out=X[:, gq, :], in0=pt[:], scalar=MOD[:, l, 40 + gq:41 + gq], in1=X[:, gq, :],
                                                                            op0=ALU.mult, op1=ALU.add), [ptk, "MOD", f"X{gq}"], [f"X{gq}"])

        XK = [f"X{kc}" for kc in range(KC)]
        xsrc = xT.rearrange("(kc p) t -> p kc t", p=128)
        odst = outT.rearrange("(kc p) t -> p kc t", p=128)
        for ti in range(NT):
            ts_ = slice(ti * TT, (ti + 1) * TT)
            P.dma("sp", "ld_x", X[:], xsrc[:, :, ts_], writes=XK)
            for l in range(L if KSTAGE >= 3 else 0):
                tile_layer(l, ti)
            if do_final:
                for kc in range(KC):
                    P.op("act", lambda e, kc=kc: e.activation(out=SQ[:, kc, :], in_=X[:, kc, :], func=AF.Square), [f"X{kc}"], [f"MIXT{kc}"])
                pt, ptk = big()
                for kc in range(KC):
                    P.op("pe", lambda e, kc=kc, pt=pt: e.matmul(pt[:], lhsT=ONESB[:], rhs=SQ[:, kc, :], start=(kc == 0), stop=(kc == KC - 1)),
                         ["ONESB", f"MIXT{kc}"], [ptk], inc=(kc == KC - 1))
                rs, rk = rstd_bc(pt, ptk, 1.0 / D, "rms")
                for kc in range(KC):
                    P.op("dve", lambda e, kc=kc, rs=rs: e.scalar_tensor_tensor(out=X[:, kc, :], in0=X[:, kc, :], scalar=FG[:, kc:kc + 1], in1=rs[:],
                                                                                op0=ALU.mult, op1=ALU.mult), [f"X{kc}", "FG", rk], [f"X{kc}"])
            P.dma("sp", "st_x", odst[:, :, ts_], X[:], reads=XK, writes=["OUT"])
        P.final_wait("sp", ["st_x"])
        if dbg:
            pass
    return nc


def _consts():
    s = np.arange(128)[:, None]
    t = np.arange(128)[None, :]
    same = (s // 64) == (t // 64)
    c = np.zeros((128, NCONST), np.float32)
    c[:, C_ID:C_ID + 128] = np.eye(128, dtype=np.float32)
    c[:, C_NMI:C_NMI + 128] = np.where(same & (s <= t), 0.0, NEG)
    c[:, C_NMS:C_NMS + 128] = np.where(same & (s < t), 0.0, NEG)
    c[:, C_MC:C_MC + 128] = np.where(same & (s <= t), 1.0, 0.0)
    c[:, C_BO:C_BO + 128] = np.where(same, 1.0, 0.0)
    c[:, C_H0:C_H0 + 128] = np.where(s < 64, 1.0, 0.0) + 0.0 * t
    c[:, C_H1:C_H1 + 128] = np.where(s >= 64, 1.0, 0.0) + 0.0 * t
    c[:, C_CM:C_CM + 128] = np.where(s <= t, 1.0, 0.0)
    c[:, C_ONE:C_ONE + 128] = 1.0
    return c


def _lmask():
    t = np.arange(128)[:, None]
    s_ = np.arange(128)[None, :]
    out = np.zeros((128, 6, 256), np.float32)
    for j in range(6):
        b = 2 ** j
        ml = ((t // (2 * b) == s_ // (2 * b)) & (t % (2 * b) >= b) & (s_ % (2 * b) < b)).astype(np.float32)
        out[:, j, 0:128] = ml.T
        out[:, j, 128:256] = ml
    return np.ascontiguousarray(out.reshape(128, 6 * 256))


def _pack_layer(l, norm1_g, norm2_g, b_ada, sgu_ln_g, sgu_ln_b, conv_w, gdn_norm_g, a_log, dt_bias, sgu_b, sgu_w):
    pp = np.zeros((128, NPP), np.float32)
    pp[:, 0:8] = norm1_g[l].reshape(8, 128).T
    pp[:, 8:16] = norm2_g[l].reshape(8, 128).T
    pp[:, 16:64] = b_ada[l].reshape(48, 128).T
    pp[:, 64:68] = sgu_ln_g[l].reshape(4, 128).T
    pp[:, 68:72] = sgu_ln_b[l].reshape(4, 128).T
    for j in range(4):
        pp[:, 72 + j * 12:72 + (j + 1) * 12] = conv_w[l, j].reshape(12, 128).T
    pp[:, 120] = gdn_norm_g[l]
    pp[:, 121:125] = np.broadcast_to(a_log[l][None, :], (128, 4))
    pp[:, 125:129] = np.broadcast_to(dt_bias[l][None, :], (128, 4))
    pbs = np.ascontiguousarray(np.broadcast_to(sgu_b[l].reshape(1, 512), (128, 512))).astype(np.float32)
    swT = np.ascontiguousarray(np.transpose(sgu_w[l], (2, 0, 1)).reshape(128, 512)).astype(np.float32)
    return pp, pbs, swT


_CACHE = {}


def _get_prog(L, T, do_final):
    key = (L, T, do_final)
    if key not in _CACHE:
        _CACHE[key] = build(L, T, do_final)
    return _CACHE[key]


FUSED = True


def kernel(x, c, w_ada, b_ada, norm1_g, w_in, sgu_ln_g, sgu_ln_b, sgu_w, sgu_b, conv_w,
           a_log, dt_bias, gdn_norm_g, w_out, norm2_g, w_ff1, w_ff2, final_g):
    args = [np.asarray(a, dtype=np.float32) for a in (x, c, w_ada, b_ada, norm1_g, w_in, sgu_ln_g, sgu_ln_b, sgu_w, sgu_b,
                                                       conv_w, a_log, dt_bias, gdn_norm_g, w_out, norm2_g, w_ff1, w_ff2, final_g)]
    (x, c, w_ada, b_ada, norm1_g, w_in, sgu_ln_g, sgu_ln_b, sgu_w, sgu_b, conv_w,
     a_log, dt_bias, gdn_norm_g, w_out, norm2_g, w_ff1, w_ff2, final_g) = args
    B, T, _ = x.shape
    depth = w_in.shape[0]
    consts = _consts()
    fgp = np.ascontiguousarray(final_g.reshape(8, 128).T)
    packs = [_pack_layer(l, norm1_g, norm2_g, b_ada, sgu_ln_g, sgu_ln_b, conv_w, gdn_norm_g, a_log, dt_bias, sgu_b, sgu_w)
             for l in range(depth)]
    xTs = [np.ascontiguousarray(x[b].T) for b in range(B)]
    cTs = [np.ascontiguousarray(c[b].reshape(8, 128).T) for b in range(B)]

    def launch(layers, xin, do_final):
        L = len(layers)
        nc = _get_prog(L, T, do_final)
        shared = {
            "w_ada": np.ascontiguousarray(w_ada[layers]), "w_in": np.ascontiguousarray(w_in[layers]),
            "w_out": np.ascontiguousarray(w_out[layers]), "w_ff1": np.ascontiguousarray(w_ff1[layers]),
            "w_ff2": np.ascontiguousarray(w_ff2[layers]),
            "pp": np.stack([packs[l][0] for l in layers]), "pbs": np.stack([packs[l][1] for l in layers]),
            "swT": np.stack([packs[l][2] for l in layers]), "fg": fgp, "consts": consts, "lmask": _lmask(),
        }
        in_maps = [dict(shared, xT=xin[b], cT=cTs[b]) for b in range(B)]
        res = run_bass_kernel_spmd(nc, in_maps, core_ids=list(range(B)))
        return [np.asarray(r["outT"]) for r in res.results]

    if FUSED:
        cur = launch(list(range(depth)), xTs, True)
    else:
        cur = xTs
        for l in range(depth):
            cur = launch([l], cur, l == depth - 1)
    out = np.stack([o.T for o in cur]).astype(np.float32)
    return out
```

```python
import os
import numpy as np
from contextlib import ExitStack
import concourse.bass as bass
import concourse.mybir as mybir
from concourse.bass_utils import run_bass_kernel_spmd

F32 = mybir.dt.float32
BF16 = mybir.dt.bfloat16
AF = mybir.ActivationFunctionType
ALU = mybir.AluOpType

D = 1024
KC = 8
TT = 512
NTB = 4
INW = 3080
NPP = 129
RMS_EPS = 1e-6
LN_EPS = 1e-5
NEG = -30000.0

C_ID, C_NMI, C_NMS, C_MC, C_BO, C_H0, C_H1, C_CM, C_ONE = [i * 128 for i in range(9)]
NCONST = 9 * 128


class Prog:
    def __init__(self, nc, es):
        self.nc = nc
        self.es = es
        self.e = {"pe": nc.tensor, "act": nc.scalar, "dve": nc.vector, "pool": nc.gpsimd, "sp": nc.sync}
        self.sem = {k: es.enter_context(nc.semaphore("s_" + k)) for k in self.e}
        self.cnt = {k: 0 for k in self.e}
        self.seen = {k: {} for k in self.e}
        self.w = {}
        self.r = {}
        self.dsem = {}
        self.dcnt = {}
        self.prefix_ok = set()
        self.n_thr = 0
        self.n_ins = 0
        self.pending = []
        self.pool_ops = 0

    def _handle(self, sk):
        return self.sem[sk] if sk in self.sem else self.dsem[sk]

    def _wait(self, eng, sk, v):
        if sk == eng and eng == "pe":
            return
        if sk in self.dsem and sk not in self.prefix_ok:
            v = self.dcnt[sk]
        if self.seen[eng].get(sk, 0) >= v:
            return
        self.e[eng].wait_ge(self._handle(sk), v)
        self.seen[eng][sk] = v

    def _deps(self, eng, reads, writes):
        deps = {}
        for k in reads:
            w = self.w.get(k)
            if w is not None:
                deps[w[0]] = max(deps.get(w[0], 0), w[1])
        for k in writes:
            w = self.w.get(k)
            if w is not None:
                deps[w[0]] = max(deps.get(w[0], 0), w[1])
            for sk, v in self.r.get(k, {}).items():
                deps[sk] = max(deps.get(sk, 0), v)
        for sk, v in deps.items():
            self._wait(eng, sk, v)

    def _mark(self, me, reads, writes):
        for k in reads:
            d = self.r.setdefault(k, {})
            d[me[0]] = max(d.get(me[0], 0), me[1])
        for k in writes:
            self.w[k] = me
            self.r[k] = {}

    def op(self, eng, fn, reads=(), writes=(), inc=True):
        self._deps(eng, reads, writes)
        ins = fn(self.e[eng])
        self.n_ins += 1
        if inc:
            self.cnt[eng] += 1
            ins.then_inc(self.sem[eng], 1)
            me = (eng, self.cnt[eng])
        else:
            me = (eng, self.cnt[eng] + 1)
        self._mark(me, reads, writes)
        if eng == "pool" and self.pending:
            self.pool_ops += 1
            if self.pool_ops % 5 == 0:
                self.pending.pop(0)[1]()
        return ins

    def flush_pending(self, upto_tag):
        while self.pending and self.pending[0][0] <= upto_tag:
            self.pending.pop(0)[1]()

    def dma(self, eng, semname, out, in_, reads=(), writes=()):
        if semname not in self.dsem:
            self.dsem[semname] = self.es.enter_context(self.nc.semaphore("d_" + semname))
            self.dcnt[semname] = 0
        self._deps(eng, reads, writes)
        ins = self.e[eng].dma_start(out=out, in_=in_)
        ins.then_inc(self.dsem[semname], 16)
        self.dcnt[semname] += 16
        self.n_ins += 1
        me = (semname, self.dcnt[semname])
        self._mark(me, reads, writes)

    def dma_throttled(self, eng, out, in_, reads=(), writes=(), depth=2):
        k = f"thr{self.n_thr % depth}"
        self.n_thr += 1
        self.prefix_ok.add(k)
        if k in self.dsem and self.dcnt[k] > 0:
            self._wait(eng, k, self.dcnt[k])
        self.dma(eng, k, out, in_, reads, writes)

    def fence(self, engs, keys):
        for eng in engs:
            self._deps(eng, (), keys)

    def barrier(self):
        for eng in self.e:
            for k in self.e:
                if self.cnt[k] > 0:
                    self._wait(eng, k, self.cnt[k])
            for k in self.dsem:
                if self.dcnt[k] > 0:
                    self._wait(eng, k, self.dcnt[k])

    def final_wait(self, eng, semnames):
        for k in semnames:
            self._wait(eng, k, self.dcnt[k])


def build(L, T, do_final, dbg=None):
    NT = T // TT
    nc = bass.Bass("TRN2", target_bir_lowering=False)

    def din(name, shape, dt=F32):
        return nc.dram_tensor(name, list(shape), dt, kind="ExternalInput").ap()

    xT = din("xT", [D, T])
    cT = din("cT", [128, KC])
    w_ada = din("w_ada", [L, D, 6 * D])
    w_in = din("w_in", [L, D, INW])
    w_out = din("w_out", [L, D, D])
    w_ff1 = din("w_ff1", [L, D, 4 * D])
    w_ff2 = din("w_ff2", [L, 4 * D, D])
    pp = din("pp", [L, 128, NPP])
    pbs = din("pbs", [L, 128, 512])
    swT = din("swT", [L, 128, 512])
    fg = din("fg", [128, KC])
    consts = din("consts", [128, NCONST])
    lmask = din("lmask", [128, 6 * 256])
    outT = nc.dram_tensor("outT", [D, T], F32, kind="ExternalOutput").ap()
    scr_in = nc.dram_tensor("scr_in", [L, 7, 128, 4096], BF16, kind="Internal").ap()
    scr_out = nc.dram_tensor("scr_out", [L, 2, 128, 4096], BF16, kind="Internal").ap()
    scr_f1 = nc.dram_tensor("scr_f1", [L, 8, 128, 4096], BF16, kind="Internal").ap()
    scr_f2 = nc.dram_tensor("scr_f2", [L, 8, 128, 4096], BF16, kind="Internal").ap()
    dbg_out = {}
    if dbg:
        for name, shape in dbg.items():
            dbg_out[name] = nc.dram_tensor("dbg_" + name, list(shape), F32, kind="ExternalOutput").ap()

    with ExitStack() as es:
        P = Prog(nc, es)

        def sb(name, shape, dt=F32):
            return es.enter_context(nc.sbuf_tensor(name, list(shape), dt))

        def ps(name, shape, dt=F32):
            return es.enter_context(nc.psum_tensor(name, list(shape), dt))

        def cast_jobs(l):
            jobs = []
            src = w_in[l].rearrange("(kc p) n -> p kc n", p=128)
            for g in range(6):
                jobs.append((scr_in[l, g].rearrange("p (kc n) -> p kc n", kc=KC), src[:, :, g * 512:(g + 1) * 512], f"scr_in{l}"))
            jobs.append((scr_in[l, 6][:, 0:64].rearrange("p (kc n) -> p kc n", kc=KC), src[:, :, 3072:3080], f"scr_in{l}"))
            src = w_out[l].rearrange("(kc p) n -> p kc n", p=128)
            for g in range(2):
                jobs.append((scr_out[l, g].rearrange("p (kc n) -> p kc n", kc=KC), src[:, :, g * 512:(g + 1) * 512], f"scr_out{l}"))
            src = w_ff1[l].rearrange("(kc p) n -> p kc n", p=128)
            for g in range(8):
                jobs.append((scr_f1[l, g].rearrange("p (kc n) -> p kc n", kc=KC), src[:, :, g * 512:(g + 1) * 512], f"scr_f1{l}"))
            src = w_ff2[l].rearrange("(kc p) n -> p kc n", p=128)
            for g in range(8):
                jobs.append((scr_f2[l, g].rearrange("p (kc n) -> p kc n", kc=32), src[:, :, g * 128:(g + 1) * 128], f"scr_f2{l}"))
            return jobs

        for l in range(L):
            for (dst, src_, key) in cast_jobs(l):
                if l == 0:
                    P.dma_throttled("pool", dst, src_, writes=[key])
                else:
                    P.pending.append((l, lambda dst=dst, src_=src_, key=key: P.dma_throttled("pool", dst, src_, writes=[key])))

        CONST = sb("CONST", [128, NCONST])
        FG = sb("FG", [128, KC])
        IDB = sb("IDB", [128, 128], BF16)
        IIB = sb("IIB", [128, 256], BF16)
        ONESB = sb("ONESB", [128, 128], BF16)
        PPS = sb("PPS", [128, L, NPP])
        MOD = sb("MOD", [128, L, 48])
        A1 = sb("A1", [128, L, KC])
        A2 = sb("A2", [128, L, KC])
        NEA16 = sb("NEA16", [128, L, 16])
        DTB16 = sb("DTB16", [128, L, 16])
        WMT = sb("WMT", [128, L, 512], BF16)
        MH = sb("MH", [128, L, 512])
        CA = sb("CA", [128, KC])
        S = sb("S", [128, L * 4, 128])
        SBF = sb("SBF", [128, L * 4, 128], BF16)
        CVT = sb("CVT", [128, L, 12, 4], BF16)
        LM = sb("LM", [128, 6, 256], BF16)

        IDF = CONST[:, C_ID:C_ID + 128]
        NMI = CONST[:, C_NMI:C_NMI + 128]
        NMS = CONST[:, C_NMS:C_NMS + 128]
        ONESF = CONST[:, C_ONE:C_ONE + 128]

        PB = [ps(f"PB{i}", [128, 512]) for i in range(6)]
        PT = [ps(f"PT{i}", [128, 1024], BF16) for i in range(2)]
        st = {"pb": 0, "pt": 0}

        def big():
            i = st["pb"]
            st["pb"] = (i + 1) % 6
            return PB[i], f"PB{i}"

        small = big

        def ptile():
            i = st["pt"]
            st["pt"] = (i + 1) % 2
            return PT[i], f"PTt{i}"

        P.dma("sp", "ld_c", CONST[:], consts, writes=["CONST"])
        P.dma("sp", "ld_c", FG[:], fg, writes=["FG"])
        P.dma("sp", "ld_c", CA[:], cT, writes=["CA"])
        for l in range(L):
            P.dma("sp", "ld_c", PPS[:, l, :], pp[l], writes=["PPS"])
        P.op("dve", lambda e: e.tensor_copy(out=IDB[:], in_=IDF), ["CONST"], ["IDB"])
        P.op("dve", lambda e: e.tensor_copy(out=IIB[:, 0:128], in_=IDF), ["CONST"], ["IIB"])
        P.op("dve", lambda e: e.tensor_copy(out=IIB[:, 128:256], in_=IDF), ["CONST"], ["IIB"])
        P.op("dve", lambda e: e.tensor_copy(out=ONESB[:], in_=ONESF), ["CONST"], ["ONESB"])
        P.op("dve", lambda e: e.memset(S[:], 0.0), [], ["S"])
        P.op("dve", lambda e: e.memset(SBF[:], 0.0), [], ["SBF"])
        P.op("dve", lambda e: e.memset(CVT[:], 0.0), [], ["CVT"])
        P.op("act", lambda e: e.activation(out=CA[:], in_=CA[:], func=AF.Silu), ["CA"], ["CA"])

        KSTAGE = int(os.environ.get('KSTAGE', '9'))
        with ExitStack() as es2:
            def sb2(name, shape, dt=F32):
                return es2.enter_context(nc.sbuf_tensor(name, list(shape), dt))
            WA = [sb2(f"WA{i}", [128, KC, 512]) for i in range(2)]
            MROW = sb2("MROW", [1, 6 * D])
            SWT = sb2("SWT", [128, 512])
            WMF = sb2("WMF", [128, 512])
            BSB = sb2("BSB", [128, 512])
            TMP4 = sb2("TMP4", [128, 4])
            LMF = sb2("LMF", [128, 6 * 256])
            P.dma("sp", "ld_lm", LMF[:], lmask, writes=["LMF"])
            P.op("dve", lambda e: e.tensor_copy(out=LM[:].rearrange("p a b -> p (a b)"), in_=LMF[:]), ["LMF"], ["LM"])
            for l in range(L if KSTAGE >= 2 else 0):
                wsrc = w_ada[l].rearrange("(kc p) n -> p kc n", p=128)
                for q in range(12):
                    wa = WA[q % 2]
                    wk = f"WA{q % 2}"
                    P.dma("sp", wk, wa[:], wsrc[:, :, q * 512:(q + 1) * 512], writes=[wk])
                    pq, pqk = big()
                    for kc in range(KC):
                        P.op("pe", lambda e, wa=wa, kc=kc, pq=pq: e.matmul(pq[0:1, :], lhsT=CA[:, kc:kc + 1], rhs=wa[:, kc, :],
                                                                          start=(kc == 0), stop=(kc == KC - 1)),
                             [wk, "CA"], [pqk], inc=(kc == KC - 1))
                    if q % 2 == 0:
                        P.op("act", lambda e, q=q, pq=pq: e.activation(out=MROW[0:1, q * 512:(q + 1) * 512], in_=pq[0:1, :], func=AF.Copy), [pqk], ["MROW"])
                    else:
                        P.op("dve", lambda e, q=q, pq=pq: e.tensor_copy(out=MROW[0:1, q * 512:(q + 1) * 512], in_=pq[0:1, :]), [pqk], ["MROW"])
                pm, pmk = big()
                for j in range(48):
                    P.op("pe", lambda e, j=j: e.matmul(pm[:, j:j + 1], lhsT=MROW[0:1, j * 128:(j + 1) * 128], rhs=CONST[0:1, C_ONE:C_ONE + 1],
                                                       start=True, stop=True), ["MROW", "CONST"], [pmk], inc=(j == 47))
                P.op("dve", lambda e: e.tensor_tensor(out=MOD[:, l, :], in0=pm[:, 0:48], in1=PPS[:, l, 16:64], op=ALU.add),
                     [pmk, "PPS"], ["MOD"])
                P.op("dve", lambda e: e.scalar_tensor_tensor(out=A1[:, l, :], in0=MOD[:, l, 8:16], scalar=1.0,
                                                              in1=PPS[:, l, 0:8], op0=ALU.add, op1=ALU.mult),
                     ["MOD", "PPS"], ["A1"])
                P.op("dve", lambda e: e.scalar_tensor_tensor(out=A2[:, l, :], in0=MOD[:, l, 32:40], scalar=1.0,
                                                              in1=PPS[:, l, 8:16], op0=ALU.add, op1=ALU.mult),
                     ["MOD", "PPS"], ["A2"])
                P.op("act", lambda e: e.activation(out=TMP4[:], in_=PPS[:, l, 121:125], func=AF.Exp), ["PPS"], ["TMP4"])
                for tb in range(4):
                    P.op("dve", lambda e, tb=tb: e.tensor_scalar(out=NEA16[:, l, tb * 4:tb * 4 + 4], in0=TMP4[:], scalar1=-1.0,
                                                                 scalar2=None, op0=ALU.mult), ["TMP4"], ["NEA16"])
                    P.op("dve", lambda e, tb=tb: e.tensor_copy(out=DTB16[:, l, tb * 4:tb * 4 + 4], in_=PPS[:, l, 125:129]),
                         ["PPS"], ["DTB16"])
                P.dma("sp", "ld_sw", SWT[:], swT[l], writes=["SWT"])
                P.dma("sp", "ld_sw", BSB[:], pbs[l], writes=["BSB"])
                for h in range(4):
                    P.op("dve", lambda e, h=h: e.tensor_tensor(out=WMF[:, h * 128:(h + 1) * 128], in0=SWT[:, h * 128:(h + 1) * 128],
                                                               in1=CONST[:, C_CM:C_CM + 128], op=ALU.mult),
                         ["SWT", "CONST"], ["WMF"])
                P.op("dve", lambda e: e.tensor_copy(out=WMT[:, l, :], in_=WMF[:]), ["WMF"], ["WMT"])
                pr, prk = big()
                P.op("pe", lambda e: e.matmul(pr[:], lhsT=ONESF, rhs=WMF[:], start=True, stop=True), ["CONST", "WMF"], [prk])
                for h in range(4):
                    P.op("dve", lambda e, h=h: e.scalar_tensor_tensor(
                        out=MH[:, l, h * 128:(h + 1) * 128], in0=pr[:, h * 128:(h + 1) * 128], scalar=PPS[:, l, 68 + h:69 + h],
                        in1=BSB[:, h * 128:(h + 1) * 128], op0=ALU.mult, op1=ALU.add), [prk, "PPS", "BSB"], ["MH"])
            P.barrier()

        X = sb("X", [128, KC, TT])
        RSB = [sb(f"RS{i}", [128, TT]) for i in range(2)]
        LNTB = [sb(f"LNT{i}", [128, TT]) for i in range(2)]
        XN = [sb(f"XN{i}", [128, TT]) for i in range(2)]
        HT = sb("HT", [128, KC, TT], BF16)
        UG = sb("UG", [128, 4, TT], BF16)
        VSG = sb("VSG", [128, TT])
        NRM = sb("NRM", [128, NTB, 512], BF16)
        ZS = sb("ZS", [128, 4, TT], BF16)
        ARENA = sb("ARENA", [128, 32 * TT], BF16)
        H1 = ARENA[:].rearrange("p (c t) -> p c t", c=32)
        CV = ARENA[:, 0:12 * 516].rearrange("p (c t) -> p c t", c=12)
        QKS = ARENA[:, 6272:6272 + 4096].rearrange("p (c t) -> p c t", c=8)
        QKN = ARENA[:, 10368:10368 + 4096].rearrange("p (c t) -> p c t", c=8)
        VT = sb("VT", [128, 4, TT], BF16)
        KTM = sb("KTM", [128, NTB, 512], BF16)
        VTM = sb("VTM", [128, NTB, 512], BF16)
        MIXT = sb("MIXT", [128, KC, TT], BF16)
        SQ = MIXT
        OT = HT
        NR = 3
        RING = sb("RING", [128, NR, 4096], BF16)
        DG = sb("DG", [128, 48, 128], BF16)
        BA4 = sb("BA4", [128, NTB, 8])
        T16 = [sb(f"T16_{i}", [128, NTB, 4]) for i in range(3)]
        BETA = sb("BETA", [128, NTB, 4])
        LNB = sb("LNB", [128, NTB, 4])
        G16 = sb("G16", [128, NTB, 4])
        GC4 = sb("GC4", [128, NTB, 16])
        GAM = sb("GAM", [128, NTB, 4])
        BGAM = sb("BGAM", [128, NTB, 4])
        KDS = sb("KDS", [128, NTB, 4])
        GLE = sb("GLE", [128, NTB, 8])
        ST6 = sb("ST6", [128, 6])
        MV = sb("MV", [128, 2])
        RSTD = sb("RSTD", [128, 1])
        NSET = 4
        gt = []
        for i in range(NSET):
            gt.append(dict(
                MG=sb(f"MG{i}", [128, 128]), MG2=sb(f"MG2{i}", [128, 128]),
                E1=sb(f"E1{i}", [128, 128]), E2=sb(f"E2{i}", [128, 128]),
                QKT=sb(f"QKT{i}", [128, 128], BF16),
                UN=[sb(f"UN{i}_{j}", [128, 256], BF16) for j in range(2)],
                PP=[sb(f"PP{i}_{j}", [128, 256], BF16) for j in range(2)],
                BV=sb(f"BV{i}", [128, 128], BF16), BGK=sb(f"BGK{i}", [128, 128], BF16),
                KDEC=sb(f"KDEC{i}", [128, 128], BF16), UNEW=sb(f"UNEW{i}", [128, 128]),
                WKT=sb(f"WKT{i}", [128, 128], BF16), GB=sb(f"GB{i}", [128, 128]),
                QDT=sb(f"QDT{i}", [128, 128], BF16), WSB=sb(f"WSB{i}", [128, 128], BF16),
            ))

        if os.environ.get("KSB"):
            print("SBUF remaining after alloc:", nc.sbuf_bytes_remaining)
        GDN_KEYS = ["CV", "QKS", "QKN"]
        H1_KEYS = [f"H1_{j}" for j in range(32)]

        seq = []
        for ti in range(NT):
            for l in range(L):
                for g in (2, 3, 4, 6, 1, 0, 5):
                    seq.append(("in", l, g))
                for g in range(2):
                    seq.append(("out", l, g))
                for g in range(8):
                    seq.append(("f1", l, g))
                for g in range(8):
                    seq.append(("f2", l, g))
        scr = {"in": scr_in, "out": scr_out, "f1": scr_f1, "f2": scr_f2}
        wst = {"issued": 0, "used": 0}

        def issue_loads(upto):
            while wst["issued"] < min(upto, len(seq)):
                n = wst["issued"]
                kind, l, g = seq[n]
                P.flush_pending(l)
                slot = n % NR
                if kind == "in" and g == 6:
                    P.dma("sp", f"ring{slot}", RING[:, slot, 0:64], scr[kind][l, g][:, 0:64],
                          reads=[f"scr_{kind}{l}"], writes=[f"RING{slot}"])
                else:
                    P.dma("sp", f"ring{slot}", RING[:, slot, :], scr[kind][l, g],
                          reads=[f"scr_{kind}{l}"], writes=[f"RING{slot}"])
                wst["issued"] += 1

        def next_piece(kind, l, g):
            n = wst["used"]
            assert seq[n] == (kind, l, g), (seq[n], kind, l, g)
            issue_loads(n + NR)
            wst["used"] += 1
            slot = n % NR
            return RING[:, slot, :], f"RING{slot}"

        rst = {"i": 0}

        def rstd_bc(pt, ptk, scale, eps, extra_bias=0.0):
            i = rst["i"]
            rst["i"] = 1 - i
            lnt, lk, rs, rk = LNTB[i], f"LNT{i}", RSB[i], f"RS{i}"
            P.op("act", lambda e: e.activation(out=lnt[:], in_=pt[:], func=AF.Ln, scale=scale, bias=EPS[eps][:, 0:1]),
                 [ptk, "EPSC"], [lk])
            if extra_bias == 0.0:
                P.op("act", lambda e: e.activation(out=rs[:], in_=lnt[:], func=AF.Exp, scale=-0.5), [lk], [rk])
            else:
                P.op("act", lambda e: e.activation(out=rs[:], in_=lnt[:], func=AF.Exp, scale=-0.5,
                                                   bias=EPS["qs"][:, 0:1]), [lk, "EPSC"], [rk])
            return rs, rk

        EPSC = sb("EPSC", [128, 4])
        EPS = {"rms": EPSC[:, 0:1], "ln": EPSC[:, 1:2], "one": EPSC[:, 2:3], "qs": EPSC[:, 3:4]}
        P.op("dve", lambda e: e.memset(EPSC[:, 0:1], RMS_EPS), [], ["EPSC"])
        P.op("dve", lambda e: e.memset(EPSC[:, 1:2], LN_EPS), [], ["EPSC"])
        for _ in range(1 + int(os.environ.get("KNONCE", "0"))):
            P.op("dve", lambda e: e.memset(EPSC[:, 2:3], 1.0), [], ["EPSC"])
        P.op("dve", lambda e: e.memset(EPSC[:, 3:4], float(-0.5 * np.log(128.0))), [], ["EPSC"])

        def norm_mod(A, SH, l):
            for kc in range(KC):
                P.op("act", lambda e, kc=kc: e.activation(out=SQ[:, kc, :], in_=X[:, kc, :], func=AF.Square),
                     [f"X{kc}"], [f"MIXT{kc}"])
            pt, ptk = big()
            for kc in range(KC):
                P.op("pe", lambda e, kc=kc: e.matmul(pt[:], lhsT=ONESB[:], rhs=SQ[:, kc, :], start=(kc == 0), stop=(kc == KC - 1)),
                     ["ONESB", f"MIXT{kc}"], [ptk], inc=(kc == KC - 1))
            rs, rk = rstd_bc(pt, ptk, 1.0 / D, "rms")
            for kc in range(KC):
                xn = XN[kc % 2]
                xk = f"XN{kc % 2}"
                P.op("dve", lambda e, kc=kc, xn=xn: e.scalar_tensor_tensor(out=xn[:], in0=X[:, kc, :], scalar=A[:, l, kc:kc + 1],
                                                                            in1=rs[:], op0=ALU.mult, op1=ALU.mult),
                     [f"X{kc}", rk, "A"], [xk])
                P.op("act", lambda e, kc=kc, xn=xn: e.activation(out=HT[:, kc, :], in_=xn[:], func=AF.Identity, bias=SH(l, kc), scale=1.0),
                     [xk, "MOD"], [f"HT{kc}"])

        def gemm_fm(wt, wk, nk, c, rhs_fn, rhs_keys, kstride):
            pt, ptk = big()
            for kc in range(nk):
                P.op("pe", lambda e, kc=kc: e.matmul(pt[:], lhsT=wt[:, kc * kstride + c * 128: kc * kstride + (c + 1) * 128],
                                                     rhs=rhs_fn(kc), start=(kc == 0), stop=(kc == nk - 1)),
                     [wk] + rhs_keys, [ptk], inc=(kc == nk - 1))
            return pt, ptk

        HTK = [f"HT{kc}" for kc in range(KC)]

        def tile_layer(l, ti):
            first = (ti == 0)
            P.fence(["act", "dve", "pool"], H1_KEYS)
            SH1 = lambda l, kc: MOD[:, l, 0 + kc:1 + kc]
            SH2 = lambda l, kc: MOD[:, l, 24 + kc:25 + kc]
            norm_mod(A1, SH1, l)
            if KSTAGE == 3:
                return
            P.op("pool", lambda e: e.tensor_copy(out=CV[:, :, 0:3], in_=CVT[:, l, :, 0:3]), ["CVT"], ["CV"])
            for j in range(4):
                for c in range(12):
                    cwj = PPS[:, l, 72 + j * 12 + c:73 + j * 12 + c]
                    if c % 2 == 0:
                        P.op("act", lambda e, j=j, c=c, cwj=cwj: e.activation(out=DG[:, j * 12 + c, :], in_=IDB[:], func=AF.Copy, scale=cwj),
                             ["IDB", "PPS"], ["DG"])
                    else:
                        P.op("pool", lambda e, j=j, c=c, cwj=cwj: e.tensor_scalar(out=DG[:, j * 12 + c, :], in0=IDB[:], scalar1=cwj,
                                                                                  scalar2=0.0, op0=ALU.mult, op1=ALU.add), ["IDB", "PPS"], ["DG"])
            n_ev = 0
            for g in (2, 3, 4):
                wt, wk = next_piece("in", l, g)
                for c in range(4):
                    pt, ptk = gemm_fm(wt, wk, KC, c, lambda kc: HT[:, kc, :], HTK, 512)
                    cc = (g - 2) * 4 + c
                    if n_ev % 2 == 0:
                        P.op("act", lambda e, cc=cc, pt=pt: e.activation(out=CV[:, cc, 3:515], in_=pt[:], func=AF.Copy), [ptk], ["CV"])
                    else:
                        P.op("dve", lambda e, cc=cc, pt=pt: e.tensor_copy(out=CV[:, cc, 3:515], in_=pt[:]), [ptk], ["CV"])
                    n_ev += 1
            wt, wk = next_piece("in", l, 6)
            for tb in range(NTB):
                pt, ptk = small()
                for kc in range(KC):
                    P.op("pe", lambda e, kc=kc, tb=tb, pt=pt: e.matmul(pt[:, 0:8], lhsT=HT[:, kc, tb * 128:(tb + 1) * 128],
                                                                       rhs=wt[:, kc * 8:(kc + 1) * 8], start=(kc == 0), stop=(kc == KC - 1)),
                         [wk] + HTK, [ptk], inc=(kc == KC - 1))
                P.op("dve", lambda e, tb=tb, pt=pt: e.tensor_copy(out=BA4[:, tb, :], in_=pt[:, 0:8]), [ptk], ["BA4"])
            wt, wk = next_piece("in", l, 1)
            for tb in range(NTB):
                pt, ptk = big()
                for kc in range(KC):
                    P.op("pe", lambda e, kc=kc, tb=tb, pt=pt: e.matmul(pt[:], lhsT=HT[:, kc, tb * 128:(tb + 1) * 128],
                                                                       rhs=wt[:, kc * 512:(kc + 1) * 512], start=(kc == 0), stop=(kc == KC - 1)),
                         [wk] + HTK, [ptk], inc=(kc == KC - 1))
                P.op("act", lambda e, pt=pt: e.activation(out=VSG[:], in_=pt[:], func=AF.Gelu), [ptk], ["VSG"])
                P.op("dve", lambda e: e.bn_stats(out=ST6[:], in_=VSG[:]), ["VSG"], ["ST6"])
                P.op("dve", lambda e: e.bn_aggr(out=MV[:], in_=ST6[:]), ["ST6"], ["MV"])
                P.op("act", lambda e: e.activation(out=RSTD[:], in_=MV[:, 1:2], func=AF.Ln, bias=EPS["ln"][:, 0:1]), ["MV", "EPSC"], ["RSTD"])
                P.op("act", lambda e: e.activation(out=RSTD[:], in_=RSTD[:], func=AF.Exp, scale=-0.5), ["RSTD"], ["RSTD"])
                P.op("dve", lambda e, tb=tb: e.tensor_scalar(out=NRM[:, tb, :], in0=VSG[:], scalar1=MV[:, 0:1], scalar2=RSTD[:, 0:1],
                                                             op0=ALU.subtract, op1=ALU.mult), ["VSG", "MV", "RSTD"], [f"NRM{tb}"])
            wt, wk = next_piece("in", l, 0)
            for c in range(4):
                pt, ptk = gemm_fm(wt, wk, KC, c, lambda kc: HT[:, kc, :], HTK, 512)
                P.op("act", lambda e, c=c, pt=pt: e.activation(out=UG[:, c, :], in_=pt[:], func=AF.Gelu), [ptk], [f"UG{c}"])
            wt, wk = next_piece("in", l, 5)
            for c in range(4):
                pt, ptk = gemm_fm(wt, wk, KC, c, lambda kc: HT[:, kc, :], HTK, 512)
                P.op("act", lambda e, c=c, pt=pt: e.activation(out=ZS[:, c, :], in_=pt[:], func=AF.Silu), [ptk], [f"ZS{c}"])

            if KSTAGE == 4:
                return
            bra = BA4[:, :, 0:4]
            ara = BA4[:, :, 4:8]
            P.op("act", lambda e: e.activation(out=T16[0][:], in_=bra, func=AF.Exp, scale=-1.0), ["BA4"], ["T16_0"])
            P.op("act", lambda e: e.activation(out=T16[0][:], in_=T16[0][:], func=AF.Ln, bias=EPS["one"][:, 0:1]), ["T16_0", "EPSC"], ["T16_0"])
            P.op("act", lambda e: e.activation(out=BETA[:], in_=T16[0][:], func=AF.Exp, scale=-1.0), ["T16_0"], ["BETA"])
            P.op("dve", lambda e: e.tensor_scalar(out=LNB[:], in0=T16[0][:], scalar1=-1.0, scalar2=None, op0=ALU.mult), ["T16_0"], ["LNB"])
            P.op("dve", lambda e: e.tensor_tensor(out=T16[1][:], in0=ara, in1=DTB16[:, l, :].rearrange("p (a b) -> p a b", a=4), op=ALU.add),
                 ["BA4", "DTB16"], ["T16_1"])
            P.op("act", lambda e: e.activation(out=T16[1][:], in_=T16[1][:], func=AF.Exp), ["T16_1"], ["T16_1"])
            P.op("act", lambda e: e.activation(out=T16[1][:], in_=T16[1][:], func=AF.Ln, bias=EPS["one"][:, 0:1]), ["T16_1", "EPSC"], ["T16_1"])
            P.op("dve", lambda e: e.tensor_tensor(out=G16[:], in0=T16[1][:], in1=NEA16[:, l, :].rearrange("p (a b) -> p a b", a=4), op=ALU.mult),
                 ["T16_1", "NEA16"], ["G16"])
            for tb in range(NTB):
                pt, ptk = small()
                for q, off in enumerate((C_MC, C_BO, C_H0, C_H1)):
                    P.op("pe", lambda e, q=q, off=off, tb=tb, pt=pt: e.matmul(pt[:, q * 4:q * 4 + 4], lhsT=CONST[:, off:off + 128],
                                                                              rhs=G16[:, tb, :], start=True, stop=True),
                         ["CONST", "G16"], [ptk], inc=(q == 3))
                P.op("dve", lambda e, tb=tb, pt=pt: e.tensor_copy(out=GC4[:, tb, :], in_=pt[:, 0:16]), [ptk], ["GC4"])
            gc_v = GC4[:, :, 0:4]
            P.op("act", lambda e: e.activation(out=GAM[:], in_=gc_v, func=AF.Exp), ["GC4"], ["GAM"])
            P.op("dve", lambda e: e.tensor_tensor(out=BGAM[:], in0=BETA[:], in1=GAM[:], op=ALU.mult), ["BETA", "GAM"], ["BGAM"])
            P.op("dve", lambda e: e.tensor_tensor(out=T16[2][:], in0=GC4[:, :, 4:8], in1=gc_v, op=ALU.subtract), ["GC4"], ["T16_2"])
            P.op("act", lambda e: e.activation(out=KDS[:], in_=T16[2][:], func=AF.Exp), ["T16_2"], ["KDS"])
            P.op("act", lambda e: e.activation(out=GLE[:], in_=GC4[:, :, 8:16], func=AF.Exp), ["GC4"], ["GLE"])

            if KSTAGE == 5:
                return
            for c in range(12):
                pt, ptk = big()
                for j in range(4):
                    P.op("pe", lambda e, c=c, j=j, pt=pt: e.matmul(pt[:], lhsT=DG[:, j * 12 + c, :], rhs=CV[:, c, j:j + 512],
                                                                   start=(j == 0), stop=(j == 3)), ["DG", "CV"], [ptk], inc=(j == 3))
                if c < 8:
                    P.op("act", lambda e, c=c, pt=pt: e.activation(out=QKS[:, c, :], in_=pt[:], func=AF.Silu), [ptk], ["QKS"])
                else:
                    P.op("act", lambda e, c=c, pt=pt: e.activation(out=VT[:, c - 8, :], in_=pt[:], func=AF.Silu), [ptk], ["VT"])
            if KSTAGE == 51:
                return
            P.op("pool", lambda e: e.tensor_copy(out=CVT[:, l, :, 0:3], in_=CV[:, :, 512:515]), ["CV"], ["CVT"])
            if KSTAGE == 52:
                return
            for c in range(8):
                P.op("act", lambda e, c=c: e.activation(out=SQ[:, c, :], in_=QKS[:, c, :], func=AF.Square), ["QKS"], [f"MIXT{c}"])
                pt, ptk = big()
                P.op("pe", lambda e, c=c, pt=pt: e.matmul(pt[:], lhsT=ONESB[:], rhs=SQ[:, c, :], start=True, stop=True),
                     ["ONESB", f"MIXT{c}"], [ptk])
                rs, rk = rstd_bc(pt, ptk, 1.0, "rms", extra_bias=(1.0 if c < 4 else 0.0))
                P.op("dve", lambda e, c=c, rs=rs: e.tensor_tensor(out=QKN[:, c, :], in0=QKS[:, c, :], in1=rs[:], op=ALU.mult),
                     ["QKS", rk], ["QKN"])
            if KSTAGE == 53:
                return
            for tb in range(NTB):
                tbs = slice(tb * 128, (tb + 1) * 128)
                pt, ptk = ptile()
                for h in range(4):
                    P.op("pe", lambda e, h=h, pt=pt, tbs=tbs: e.transpose(out=pt[:, h * 128:(h + 1) * 128], in_=QKN[:, 4 + h, tbs], identity=IDB[:]),
                         ["QKN", "IDB"], [ptk], inc=(h == 3))
                P.op("act", lambda e, tb=tb, pt=pt: e.activation(out=KTM[:, tb, :], in_=pt[:, 0:512], func=AF.Copy), [ptk], ["KTM"])
                pt, ptk = ptile()
                for h in range(4):
                    P.op("pe", lambda e, h=h, pt=pt, tbs=tbs: e.transpose(out=pt[:, h * 128:(h + 1) * 128], in_=VT[:, h, tbs], identity=IDB[:]),
                         ["VT", "IDB"], [ptk], inc=(h == 3))
                P.op("dve", lambda e, tb=tb, pt=pt: e.tensor_copy(out=VTM[:, tb, :], in_=pt[:, 0:512]), [ptk], ["VTM"])

            if KSTAGE == 6:
                return
            for tb in range(NTB):
                tbs = slice(tb * 128, (tb + 1) * 128)
                for h in range(4):
                    hs = slice(h * 128, (h + 1) * 128)
                    pt, ptk = small()
                    P.op("pe", lambda e, pt=pt, tb=tb, hs=hs: e.matmul(pt[:, 0:128], lhsT=NRM[:, tb, hs], rhs=WMT[:, l, hs], start=True, stop=True),
                         [f"NRM{tb}", "WMT"], [ptk])
                    xn = XN[h % 2]
                    xk = f"XN{h % 2}"
                    P.op("dve", lambda e, pt=pt, h=h, hs=hs, xn=xn: e.scalar_tensor_tensor(out=xn[:, 0:128], in0=pt[:, 0:128], scalar=PPS[:, l, 64 + h:65 + h],
                                                                                            in1=MH[:, l, hs], op0=ALU.mult, op1=ALU.add),
                         [ptk, "PPS", "MH"], [xk])
                    P.op("pool", lambda e, h=h, tbs=tbs, xn=xn: e.tensor_tensor(out=MIXT[:, h, tbs], in0=xn[:, 0:128], in1=UG[:, h, tbs], op=ALU.mult),
                         [xk, f"UG{h}"], [f"MIXT{h}"])

            if KSTAGE == 7:
                return
            def gdn_chain(tb, h):
                g = gt[h]
                sfx = f"_{h}"
                tbs = slice(tb * 128, (tb + 1) * 128)
                hs = slice(h * 128, (h + 1) * 128)
                kT = QKN[:, 4 + h, tbs]
                qT = QKN[:, h, tbs]
                sc = lambda t: t[:, tb, h:h + 1]
                un = g["UN"][0]
                P.op("dve", lambda e: e.tensor_scalar(out=g["MG"][:], in0=CONST[:, C_MC:C_MC + 128], scalar1=sc(G16), scalar2=None, op0=ALU.mult),
                     ["CONST", "G16"], ["MG" + sfx])
                P.op("dve", lambda e: e.scalar_tensor_tensor(out=g["MG2"][:], in0=IDF, scalar=sc(LNB), in1=g["MG"][:], op0=ALU.mult, op1=ALU.add),
                     ["CONST", "LNB", "MG" + sfx], ["MG2" + sfx])
                P.op("act", lambda e: e.activation(out=g["BV"][:], in_=VTM[:, tb, hs], func=AF.Copy, scale=sc(BETA)),
                     ["VTM", "BETA"], ["BV" + sfx])
                P.op("act", lambda e: e.activation(out=g["BGK"][:], in_=KTM[:, tb, hs], func=AF.Copy, scale=sc(BGAM)),
                     ["KTM", "BGAM"], ["BGK" + sfx])
                P.op("pool", lambda e: e.tensor_scalar(out=g["KDEC"][:], in0=KTM[:, tb, hs], scalar1=sc(KDS), scalar2=0.0, op0=ALU.mult, op1=ALU.add),
                     ["KTM", "KDS"], ["KDEC" + sfx])
                yield
                pa, pak = small()
                pk, pkk = pa[:, 256:512], pak
                P.op("pe", lambda e: e.matmul(pa[:, 0:128], lhsT=ONESF, rhs=g["MG"][:], start=True, stop=True), ["CONST", "MG" + sfx], [pak], inc=False)
                P.op("pe", lambda e: e.matmul(pa[:, 128:256], lhsT=ONESF, rhs=g["MG2"][:], start=True, stop=True), ["CONST", "MG2" + sfx], [pak], inc=False)
                P.op("pe", lambda e: e.matmul(pk[:, 0:128], lhsT=kT, rhs=kT, start=True, stop=True), ["QKN"], [pkk], inc=False)
                P.op("pe", lambda e: e.matmul(pk[:, 128:256], lhsT=kT, rhs=qT, start=True, stop=True), ["QKN"], [pkk])
                yield
                gcs = GC4[:, tb, h:h + 1]
                P.op("dve", lambda e: e.scalar_tensor_tensor(out=g["E1"][:], in0=pa[:, 0:128], scalar=gcs, in1=NMI, op0=ALU.subtract, op1=ALU.min),
                     [pak, "GC4", "CONST"], ["E1" + sfx])
                P.op("dve", lambda e: e.scalar_tensor_tensor(out=g["E2"][:], in0=pa[:, 128:256], scalar=gcs, in1=NMS, op0=ALU.subtract, op1=ALU.min),
                     [pak, "GC4", "CONST"], ["E2" + sfx])
                P.op("dve", lambda e: e.tensor_copy(out=g["GB"][:], in_=pa[:, 0:128]), [pak], ["GB" + sfx])
                yield
                P.op("act", lambda e: e.activation(out=g["E1"][:], in_=g["E1"][:], func=AF.Exp), ["E1" + sfx], ["E1" + sfx])
                P.op("act", lambda e: e.activation(out=g["E2"][:], in_=g["E2"][:], func=AF.Exp), ["E2" + sfx], ["E2" + sfx])
                P.op("act", lambda e: e.activation(out=g["GB"][:], in_=g["GB"][:], func=AF.Exp), ["GB" + sfx], ["GB" + sfx])
                yield
                P.op("dve", lambda e: e.scalar_tensor_tensor(out=un[:, 0:128], in0=pk[:, 0:128], scalar=-1.0, in1=g["E2"][:], op0=ALU.mult, op1=ALU.mult),
                     [pkk, "E2" + sfx], ["UN0" + sfx])
                P.op("dve", lambda e: e.tensor_tensor(out=g["QKT"][:], in0=pk[:, 128:256], in1=g["E1"][:], op=ALU.mult), [pkk, "E1" + sfx], ["QKT" + sfx])
                P.op("pool", lambda e: e.tensor_tensor(out=g["QDT"][:], in0=qT, in1=g["GB"][:], op=ALU.mult), ["QKN", "GB" + sfx], ["QDT" + sfx])
                yield
                ptt, pttk = small()
                P.op("pe", lambda e: e.matmul(ptt[:, 0:128], lhsT=un[:, 0:128], rhs=IDB[:], start=True, stop=True), ["UN0" + sfx, "IDB"], [pttk])
                yield
                P.op("act", lambda e: e.activation(out=un[:, 128:256], in_=ptt[:, 0:128], func=AF.Copy), [pttk], ["UN0" + sfx])
                yield
                ppc = g["PP"][0]
                y0 = g["UN"][1]
                P.op("pool", lambda e: e.tensor_tensor(out=y0[:], in0=un[:], in1=LM[:, 0, :], op=ALU.mult), ["UN0" + sfx, "LM"], ["UN1" + sfx])
                P.op("pool", lambda e: e.tensor_tensor(out=ppc[:], in0=y0[:], in1=IIB[:], op=ALU.add), ["UN1" + sfx, "IIB"], ["PP0" + sfx])
                yield
                cur = 0
                for j in range(1, 6):
                    last = (j == 5)
                    tv, tvk = g["PP"][cur], f"PP{cur}" + sfx
                    tn, tnk = g["PP"][1 - cur], f"PP{1 - cur}" + sfx
                    yb, ybk = g["UN"][1], "UN1" + sfx
                    w_ = 128 if last else 256
                    p1, p1k = small()
                    P.op("pe", lambda e: e.matmul(p1[:, 0:128], lhsT=un[:, 128:256], rhs=tv[:, 0:128], start=True, stop=True),
                         ["UN0" + sfx, tvk], [p1k], inc=last)
                    if not last:
                        P.op("pe", lambda e: e.matmul(p1[:, 128:256], lhsT=un[:, 0:128], rhs=tv[:, 128:256], start=True, stop=True),
                             ["UN0" + sfx, tvk], [p1k])
                    yield
                    P.op("dve", lambda e: e.tensor_tensor(out=yb[:, 0:w_], in0=p1[:, 0:w_], in1=LM[:, j, 0:w_], op=ALU.mult), [p1k, "LM"], [ybk])
                    yield
                    p2, p2k = small()
                    P.op("pe", lambda e: e.matmul(p2[:, 0:128], lhsT=tv[:, 128:256], rhs=yb[:, 0:128], start=True, stop=True),
                         [tvk, ybk], [p2k], inc=last)
                    if not last:
                        P.op("pe", lambda e: e.matmul(p2[:, 128:256], lhsT=tv[:, 0:128], rhs=yb[:, 128:256], start=True, stop=True),
                             [tvk, ybk], [p2k])
                    yield
                    P.op("dve", lambda e: e.tensor_tensor(out=tn[:, 0:w_], in0=p2[:, 0:w_], in1=tv[:, 0:w_], op=ALU.add), [p2k, tvk], [tnk])
                    yield
                    cur = 1 - cur
                pu, puk = g["PP"][cur], f"PP{cur}" + sfx
                psu, psuk = small()
                P.op("pe", lambda e: e.matmul(psu[:, 0:128], lhsT=pu[:, 0:128], rhs=g["BV"][:], start=True, stop=True), [puk, "BV" + sfx], [psuk], inc=False)
                P.op("pe", lambda e: e.matmul(psu[:, 128:256], lhsT=g["BGK"][:], rhs=pu[:, 0:128], start=True, stop=True), [puk, "BGK" + sfx], [psuk])
                yield
                P.op("act", lambda e: e.activation(out=g["UNEW"][:], in_=psu[:, 0:128], func=AF.Copy), [psuk], ["UNEW" + sfx])
                P.op("act", lambda e: e.activation(out=g["WKT"][:], in_=psu[:, 128:256], func=AF.Copy), [psuk], ["WKT" + sfx])
                yield
                sidx = l * 4 + h
                skey = f"S{sidx}"
                sbk = f"SBF{sidx}"
                for c in range(2):
                    R = slice(c * 64, (c + 1) * 64)
                    p1, p1k = small()
                    P.op("pe", lambda e: e.matmul(p1[R, 0:128], lhsT=g["WKT"][:, R], rhs=SBF[:, sidx, :], start=True, stop=True),
                         ["WKT" + sfx, sbk], [p1k])
                    yield
                    P.op("dve", lambda e: e.tensor_tensor(out=g["WSB"][R, :], in0=g["UNEW"][R, :], in1=p1[R, 0:128], op=ALU.subtract),
                         ["UNEW" + sfx, p1k], ["WSB" + sfx])
                    yield
                    p2, p2k = small()
                    P.op("pe", lambda e: e.matmul(p2[:, 0:64], lhsT=SBF[:, sidx, :], rhs=g["QDT"][:, R], start=True, stop=False),
                         [sbk, "QDT" + sfx], [p2k], inc=False)
                    P.op("pe", lambda e: e.matmul(p2[:, 0:64], lhsT=g["WSB"][R, :], rhs=g["QKT"][R, R], start=False, stop=True),
                         ["WSB" + sfx, "QKT" + sfx], [p2k])
                    yield
                    P.op("act", lambda e: e.activation(out=OT[:, h, tb * 128 + c * 64: tb * 128 + (c + 1) * 64], in_=p2[:, 0:64], func=AF.Copy),
                         [p2k], [f"HT{h}"])
                    p3, p3k = small()
                    P.op("pe", lambda e: e.matmul(p3[:, 0:128], lhsT=g["KDEC"][R, :], rhs=g["WSB"][R, :], start=True, stop=True),
                         ["KDEC" + sfx, "WSB" + sfx], [p3k])
                    yield
                    P.op("dve", lambda e: e.scalar_tensor_tensor(out=S[:, sidx, :], in0=S[:, sidx, :], scalar=GLE[:, tb, c * 4 + h:c * 4 + h + 1],
                                                                  in1=p3[:, 0:128], op0=ALU.mult, op1=ALU.add),
                         [skey, "GLE", p3k], [skey])
                    yield
                    P.op("pool", lambda e: e.tensor_copy(out=SBF[:, sidx, :], in_=S[:, sidx, :]), [skey], [sbk])
                    yield

            for tb in range(NTB):
                alive = [gdn_chain(tb, h) for h in range(4)]
                while alive:
                    nxt = []
                    for gn in alive:
                        try:
                            next(gn)
                            nxt.append(gn)
                        except StopIteration:
                            pass
                    alive = nxt

            if KSTAGE == 8:
                return
            for h in range(4):
                P.op("act", lambda e, h=h: e.activation(out=SQ[:, 4 + h, :], in_=OT[:, h, :], func=AF.Square), [f"HT{h}"], [f"MIXT{4 + h}"])
                pt, ptk = big()
                P.op("pe", lambda e, h=h, pt=pt: e.matmul(pt[:], lhsT=ONESB[:], rhs=SQ[:, 4 + h, :], start=True, stop=True), ["ONESB", f"MIXT{4 + h}"], [ptk])
                rs, rk = rstd_bc(pt, ptk, 1.0 / 128, "rms")
                xn = XN[h % 2]
                xk = f"XN{h % 2}"
                P.op("dve", lambda e, h=h, xn=xn, rs=rs: e.scalar_tensor_tensor(out=xn[:], in0=OT[:, h, :], scalar=PPS[:, l, 120:121], in1=rs[:],
                                                                                 op0=ALU.mult, op1=ALU.mult), [f"HT{h}", "PPS", rk], [xk])
                P.op("pool", lambda e, h=h, xn=xn: e.tensor_tensor(out=MIXT[:, 4 + h, :], in0=xn[:], in1=ZS[:, h, :], op=ALU.mult),
                     [xk, f"ZS{h}"], [f"MIXT{4 + h}"])

            MIXK = [f"MIXT{k}" for k in range(KC)]
            for gq in range(2):
                wt, wk = next_piece("out", l, gq)
                for c in range(4):
                    j = gq * 4 + c
                    pt, ptk = gemm_fm(wt, wk, KC, c, lambda kc: MIXT[:, kc, :], MIXK, 512)
                    P.op("dve", lambda e, j=j, pt=pt: e.scalar_tensor_tensor(out=X[:, j, :], in0=pt[:], scalar=MOD[:, l, 16 + j:17 + j], in1=X[:, j, :],
                                                                              op0=ALU.mult, op1=ALU.add), [ptk, "MOD", f"X{j}"], [f"X{j}"])
            norm_mod(A2, SH2, l)
            P.fence(["act", "pool"], GDN_KEYS)
            for gq in range(8):
                wt, wk = next_piece("f1", l, gq)
                for c in range(4):
                    j = gq * 4 + c
                    pt, ptk = gemm_fm(wt, wk, KC, c, lambda kc: HT[:, kc, :], HTK, 512)
                    P.op("act", lambda e, pt=pt, j=j: e.activation(out=H1[:, j, :], in_=pt[:], func=AF.Relu), [ptk], [f"H1_{j}"])
                    if j % 2 == 0:
                        P.op("act", lambda e, j=j: e.activation(out=H1[:, j, :], in_=H1[:, j, :], func=AF.Square), [f"H1_{j}"], [f"H1_{j}"])
                    else:
                        P.op("pool", lambda e, j=j: e.tensor_tensor(out=H1[:, j, :], in0=H1[:, j, :], in1=H1[:, j, :], op=ALU.mult), [f"H1_{j}"], [f"H1_{j}"])
            for gq in range(8):
                wt, wk = next_piece("f2", l, gq)
                pt, ptk = big()
                for kc in range(32):
                    P.op("pe", lambda e, kc=kc, pt=pt: e.matmul(pt[:], lhsT=wt[:, kc * 128:(kc + 1) * 128], rhs=H1[:, kc, :],
                                                                start=(kc == 0), stop=(kc == 31)), [wk, f"H1_{kc}"], [ptk], inc=(kc == 31))
                P.op("dve", lambda e, gq=gq, pt=pt: e.scalar_tensor_tensor(out=X[:, gq, :], in0=pt[:], scalar=MOD[:, l, 40 + gq:41 + gq], in1=X[:, gq, :],
                                                                            op0=ALU.mult, op1=ALU.add), [ptk, "MOD", f"X{gq}"], [f"X{gq}"])

        XK = [f"X{kc}" for kc in range(KC)]
        xsrc = xT.rearrange("(kc p) t -> p kc t", p=128)
        odst = outT.rearrange("(kc p) t -> p kc t", p=128)
        for ti in range(NT):
            ts_ = slice(ti * TT, (ti + 1) * TT)
            P.dma("sp", "ld_x", X[:], xsrc[:, :, ts_], writes=XK)
            for l in range(L if KSTAGE >= 3 else 0):
                tile_layer(l, ti)
            if do_final:
                for kc in range(KC):
                    P.op("act", lambda e, kc=kc: e.activation(out=SQ[:, kc, :], in_=X[:, kc, :], func=AF.Square), [f"X{kc}"], [f"MIXT{kc}"])
                pt, ptk = big()
                for kc in range(KC):
                    P.op("pe", lambda e, kc=kc, pt=pt: e.matmul(pt[:], lhsT=ONESB[:], rhs=SQ[:, kc, :], start=(kc == 0), stop=(kc == KC - 1)),
                         ["ONESB", f"MIXT{kc}"], [ptk], inc=(kc == KC - 1))
                rs, rk = rstd_bc(pt, ptk, 1.0 / D, "rms")
                for kc in range(KC):
                    P.op("dve", lambda e, kc=kc, rs=rs: e.scalar_tensor_tensor(out=X[:, kc, :], in0=X[:, kc, :], scalar=FG[:, kc:kc + 1], in1=rs[:],
                                                                                op0=ALU.mult, op1=ALU.mult), [f"X{kc}", "FG", rk], [f"X{kc}"])
            P.dma("sp", "st_x", odst[:, :, ts_], X[:], reads=XK, writes=["OUT"])
        P.final_wait("sp", ["st_x"])
        if dbg:
            pass
    return nc


def _consts():
    s = np.arange(128)[:, None]
    t = np.arange(128)[None, :]
    same = (s // 64) == (t // 64)
    c = np.zeros((128, NCONST), np.float32)
    c[:, C_ID:C_ID + 128] = np.eye(128, dtype=np.float32)
    c[:, C_NMI:C_NMI + 128] = np.where(same & (s <= t), 0.0, NEG)
    c[:, C_NMS:C_NMS + 128] = np.where(same & (s < t), 0.0, NEG)
    c[:, C_MC:C_MC + 128] = np.where(same & (s <= t), 1.0, 0.0)
    c[:, C_BO:C_BO + 128] = np.where(same, 1.0, 0.0)
    c[:, C_H0:C_H0 + 128] = np.where(s < 64, 1.0, 0.0) + 0.0 * t
    c[:, C_H1:C_H1 + 128] = np.where(s >= 64, 1.0, 0.0) + 0.0 * t
    c[:, C_CM:C_CM + 128] = np.where(s <= t, 1.0, 0.0)
    c[:, C_ONE:C_ONE + 128] = 1.0
    return c


def _lmask():
    t = np.arange(128)[:, None]
    s_ = np.arange(128)[None, :]
    out = np.zeros((128, 6, 256), np.float32)
    for j in range(6):
        b = 2 ** j
        ml = ((t // (2 * b) == s_ // (2 * b)) & (t % (2 * b) >= b) & (s_ % (2 * b) < b)).astype(np.float32)
        out[:, j, 0:128] = ml.T
        out[:, j, 128:256] = ml
    return np.ascontiguousarray(out.reshape(128, 6 * 256))


def _pack_layer(l, norm1_g, norm2_g, b_ada, sgu_ln_g, sgu_ln_b, conv_w, gdn_norm_g, a_log, dt_bias, sgu_b, sgu_w):
    pp = np.zeros((128, NPP), np.float32)
    pp[:, 0:8] = norm1_g[l].reshape(8, 128).T
    pp[:, 8:16] = norm2_g[l].reshape(8, 128).T
    pp[:, 16:64] = b_ada[l].reshape(48, 128).T
    pp[:, 64:68] = sgu_ln_g[l].reshape(4, 128).T
    pp[:, 68:72] = sgu_ln_b[l].reshape(4, 128).T
    for j in range(4):
        pp[:, 72 + j * 12:72 + (j + 1) * 12] = conv_w[l, j].reshape(12, 128).T
    pp[:, 120] = gdn_norm_g[l]
    pp[:, 121:125] = np.broadcast_to(a_log[l][None, :], (128, 4))
    pp[:, 125:129] = np.broadcast_to(dt_bias[l][None, :], (128, 4))
    pbs = np.ascontiguousarray(np.broadcast_to(sgu_b[l].reshape(1, 512), (128, 512))).astype(np.float32)
    swT = np.ascontiguousarray(np.transpose(sgu_w[l], (2, 0, 1)).reshape(128, 512)).astype(np.float32)
    return pp, pbs, swT


_CACHE = {}


def _get_prog(L, T, do_final):
    key = (L, T, do_final)
    if key not in _CACHE:
        _CACHE[key] = build(L, T, do_final)
    return _CACHE[key]


FUSED = True


def kernel(x, c, w_ada, b_ada, norm1_g, w_in, sgu_ln_g, sgu_ln_b, sgu_w, sgu_b, conv_w,
           a_log, dt_bias, gdn_norm_g, w_out, norm2_g, w_ff1, w_ff2, final_g):
    args = [np.asarray(a, dtype=np.float32) for a in (x, c, w_ada, b_ada, norm1_g, w_in, sgu_ln_g, sgu_ln_b, sgu_w, sgu_b,
                                                       conv_w, a_log, dt_bias, gdn_norm_g, w_out, norm2_g, w_ff1, w_ff2, final_g)]
    (x, c, w_ada, b_ada, norm1_g, w_in, sgu_ln_g, sgu_ln_b, sgu_w, sgu_b, conv_w,
     a_log, dt_bias, gdn_norm_g, w_out, norm2_g, w_ff1, w_ff2, final_g) = args
    B, T, _ = x.shape
    depth = w_in.shape[0]
    consts = _consts()
    fgp = np.ascontiguousarray(final_g.reshape(8, 128).T)
    packs = [_pack_layer(l, norm1_g, norm2_g, b_ada, sgu_ln_g, sgu_ln_b, conv_w, gdn_norm_g, a_log, dt_bias, sgu_b, sgu_w)
             for l in range(depth)]
    xTs = [np.ascontiguousarray(x[b].T) for b in range(B)]
    cTs = [np.ascontiguousarray(c[b].reshape(8, 128).T) for b in range(B)]

    def launch(layers, xin, do_final):
        L = len(layers)
        nc = _get_prog(L, T, do_final)
        shared = {
            "w_ada": np.ascontiguousarray(w_ada[layers]), "w_in": np.ascontiguousarray(w_in[layers]),
            "w_out": np.ascontiguousarray(w_out[layers]), "w_ff1": np.ascontiguousarray(w_ff1[layers]),
            "w_ff2": np.ascontiguousarray(w_ff2[layers]),
            "pp": np.stack([packs[l][0] for l in layers]), "pbs": np.stack([packs[l][1] for l in layers]),
            "swT": np.stack([packs[l][2] for l in layers]), "fg": fgp, "consts": consts, "lmask": _lmask(),
        }
        in_maps = [dict(shared, xT=xin[b], cT=cTs[b]) for b in range(B)]
        res = run_bass_kernel_spmd(nc, in_maps, core_ids=list(range(B)))
        return [np.asarray(r["outT"]) for r in res.results]

    if FUSED:
        cur = launch(list(range(depth)), xTs, True)
    else:
        cur = xTs
        for l in range(depth):
            cur = launch([l], cur, l == depth - 1)
    out = np.stack([o.T for o in cur]).astype(np.float32)
    return out
```
